# Optimizing a Trainium2 kernel written in Bass

```python
import jax
import jax.numpy as jnp
from jax import lax
import numpy as np


D_MODEL = 2048
BATCH = 4
SEQ = 8192
DEPTH = 1
DEC_BATCH = 2
DEC_SEQ = 16384
PAST_LEN = 128

D_RNN = D_MODEL
RNN_BLOCKS = 16
RNN_BLOCK_W = D_RNN // RNN_BLOCKS
CONV_W = 4
CONV_LEFT = 2
LRU_C = 8.0
HEAD_DIM = 128
ATTN_PATTERNS = ((128, 1), (512, 4), (2048, 16))
N_GROUPS = 3
HEADS_PER_GROUP = 4
N_HEADS = N_GROUPS * HEADS_PER_GROUP
ATTN_W = N_HEADS * HEAD_DIM
ATTN_OUT_W = HEADS_PER_GROUP * HEAD_DIM
ATTN_BLOCK = 64
ROPE_DIM = HEAD_DIM // 4
ROPE_THETA = 500000.0
D_FF = 4 * D_MODEL
EPS = 1e-6
NEG_INF = -1e30
IN_COLS = 2 * D_RNN + 3 * ATTN_W + 2 * D_MODEL

kernel_name = 'hybrid_rglru_dilated_attn_encoder'


def rmsnorm(x, g):
    xf = x.astype(jnp.float32)
    y = xf * lax.rsqrt(jnp.mean(xf * xf, axis=-1, keepdims=True) + EPS)
    return (y * g.astype(jnp.float32)).astype(x.dtype)


def partial_rope(t):
    S = t.shape[1]
    half = ROPE_DIM // 2
    inv_freq = ROPE_THETA ** (-jnp.arange(0, ROPE_DIM, 2, dtype=jnp.float32) / ROPE_DIM)
    ang = jnp.arange(S, dtype=jnp.float32)[:, None] * inv_freq[None, :]
    cos = jnp.cos(ang)[None, :, None, :]
    sin = jnp.sin(ang)[None, :, None, :]
    tf = t.astype(jnp.float32)
    t1 = tf[..., :half]
    t2 = tf[..., half:ROPE_DIM]
    out = jnp.concatenate([t1 * cos - t2 * sin, t2 * cos + t1 * sin, tf[..., ROPE_DIM:]], axis=-1)
    return out.astype(t.dtype)


def dilated_window_attention(q, k, v, window, dil):
    B, S, H, E = q.shape
    R = window // (2 * dil)
    assert R <= ATTN_BLOCK
    L = S // dil
    nb = -(-L // ATTN_BLOCK)
    Lp = nb * ATTN_BLOCK

    def by_residue(t):
        return t.reshape(B, L, dil, H, E).transpose(0, 2, 3, 1, 4)

    qs, ks, vs = by_residue(q), by_residue(k), by_residue(v)
    qb = jnp.pad(qs, ((0, 0), (0, 0), (0, 0), (0, Lp - L), (0, 0))).reshape(B, dil, H, nb, ATTN_BLOCK, E)
    pad_kv = ((0, 0), (0, 0), (0, 0), (ATTN_BLOCK, Lp - L + ATTN_BLOCK), (0, 0))
    kp = jnp.pad(ks, pad_kv).reshape(B, dil, H, nb + 2, ATTN_BLOCK, E)
    vp = jnp.pad(vs, pad_kv).reshape(B, dil, H, nb + 2, ATTN_BLOCK, E)
    kw = jnp.concatenate([kp[:, :, :, :-2], kp[:, :, :, 1:-1], kp[:, :, :, 2:]], axis=4)
    vw = jnp.concatenate([vp[:, :, :, :-2], vp[:, :, :, 1:-1], vp[:, :, :, 2:]], axis=4)

    scale = 1.0 / float(np.sqrt(E))
    s = jnp.einsum('bdhnqe,bdhnke->bdhnqk', qb, kw, preferred_element_type=jnp.float32) * scale
    qi = jnp.arange(ATTN_BLOCK)[:, None]
    ki = jnp.arange(3 * ATTN_BLOCK)[None, :]
    off = ki - ATTN_BLOCK - qi
    kpos = jnp.arange(nb)[:, None, None] * ATTN_BLOCK - ATTN_BLOCK + ki[None]
    mask = (jnp.abs(off) <= R)[None] & (kpos >= 0) & (kpos < L)
    s = jnp.where(mask, s, NEG_INF)
    m = jnp.max(s, axis=-1, keepdims=True)
    p = jnp.exp(s - m)
    den = jnp.sum(p, axis=-1, keepdims=True)
    o = jnp.einsum('bdhnqk,bdhnke->bdhnqe', p, vw.astype(jnp.float32)) / den
    lse = (m + jnp.log(den))[..., 0]
    o = o.reshape(B, dil, H, Lp, E)[:, :, :, :L].transpose(0, 3, 1, 2, 4).reshape(B, S, H, E)
    lse = lse.reshape(B, dil, H, Lp)[:, :, :, :L].transpose(0, 3, 1, 2).reshape(B, S, H)
    return o, lse


def centered_conv(x, w, b):
    S = x.shape[1]
    xp = jnp.pad(x, ((0, 0), (CONV_LEFT, CONV_W - 1 - CONV_LEFT), (0, 0)))
    return b + sum(xp[:, j:j + S] * w[j] for j in range(CONV_W))


def _lin_comb(e1, e2):
    a1, b1 = e1
    a2, b2 = e2
    return a1 * a2, a2 * b1 + b2


def bidir_rg_lru(xc, w_a, b_a, w_i, b_i, lam):
    B, S, _ = xc.shape
    xb = xc.reshape(B, S, RNN_BLOCKS, RNN_BLOCK_W)
    xf = xc.astype(jnp.float32)
    hs = []
    for dr in range(2):
        r = jax.nn.sigmoid((jnp.einsum('bshi,hij->bshj', xb, w_a[dr]).reshape(B, S, D_RNN) + b_a[dr]).astype(jnp.float32))
        i = jax.nn.sigmoid((jnp.einsum('bshi,hij->bshj', xb, w_i[dr]).reshape(B, S, D_RNN) + b_i[dr]).astype(jnp.float32))
        log_a = -LRU_C * r * jax.nn.softplus(-lam[dr].astype(jnp.float32))
        a = jnp.exp(log_a)
        u = jnp.sqrt(-jnp.expm1(2.0 * log_a)) * (i * xf)
        _, h = lax.associative_scan(_lin_comb, (a, u), axis=1, reverse=(dr == 1))
        hs.append(h)
    return hs[0] + hs[1]


def encoder_layer(x, mix_g, w_in, conv_w, conv_b, w_a, b_a, w_i, b_i, lam,
                  w_rnn_proj, w_attn_proj, w_out, mlp_g, w_up, w_down):
    B, S, _ = x.shape
    xn = rmsnorm(x, mix_g)
    proj = xn @ w_in
    c0 = 2 * D_RNN
    cuts = [D_RNN, c0, c0 + ATTN_W, c0 + 2 * ATTN_W, c0 + 3 * ATTN_W, c0 + 3 * ATTN_W + D_MODEL]
    x_rnn, g_rnn, q, k, v, gate_rnn, gate_attn = jnp.split(proj, cuts, axis=-1)

    xc = centered_conv(x_rnn, conv_w, conv_b)
    h = bidir_rg_lru(xc, w_a, b_a, w_i, b_i, lam)
    y_rnn = (h * jax.nn.gelu(g_rnn.astype(jnp.float32))).astype(x.dtype) @ w_rnn_proj

    q = partial_rope(q.reshape(B, S, N_HEADS, HEAD_DIM))
    k = partial_rope(k.reshape(B, S, N_HEADS, HEAD_DIM))
    v = v.reshape(B, S, N_HEADS, HEAD_DIM)
    outs, lses = [], []
    for g, (window, dil) in enumerate(ATTN_PATTERNS):
        hsl = slice(g * HEADS_PER_GROUP, (g + 1) * HEADS_PER_GROUP)
        o, lse = dilated_window_attention(q[:, :, hsl], k[:, :, hsl], v[:, :, hsl], window, dil)
        outs.append(o)
        lses.append(lse)
    wts = jax.nn.softmax(jnp.stack(lses), axis=0)
    attn = jnp.einsum('gbsh,gbshe->bshe', wts, jnp.stack(outs)).reshape(B, S, ATTN_OUT_W).astype(x.dtype)
    y_attn = attn @ w_attn_proj

    mixed = jax.nn.sigmoid(gate_rnn) * y_rnn + jax.nn.sigmoid(gate_attn) * y_attn
    h1 = x + mixed @ w_out

    u = rmsnorm(h1, mlp_g) @ w_up
    return h1 + jnp.square(jax.nn.relu(u)) @ w_down


def setup_inputs(seed: int = 0) -> dict:
    key = jax.random.key(seed)
    ks = jax.random.split(key, 20)
    f32 = jnp.float32

    def nrm(k, shape, fan_in):
        return jax.random.normal(k, shape, f32) * (fan_in ** -0.5)

    u = jax.random.uniform(ks[9], (DEPTH, 2, D_RNN), f32, minval=0.9, maxval=0.999)
    s = u ** (1.0 / LRU_C)
    lam = jnp.log(s) - jnp.log1p(-s)
    return {
        'x_prompt': jax.random.normal(ks[0], (BATCH, SEQ, D_MODEL), f32),
        'x_sample': jax.random.normal(ks[1], (DEC_BATCH, DEC_SEQ, D_MODEL), f32),
        'mix_norm_g': 1.0 + 0.02 * jax.random.normal(ks[2], (DEPTH, D_MODEL), f32),
        'w_in': nrm(ks[3], (DEPTH, D_MODEL, IN_COLS), D_MODEL),
        'conv_w': nrm(ks[4], (DEPTH, CONV_W, D_RNN), CONV_W),
        'conv_b': 0.01 * jax.random.normal(ks[5], (DEPTH, D_RNN), f32),
        'lru_w_a': nrm(ks[6], (DEPTH, 2, RNN_BLOCKS, RNN_BLOCK_W, RNN_BLOCK_W), RNN_BLOCK_W),
        'lru_b_a': 0.01 * jax.random.normal(ks[7], (DEPTH, 2, D_RNN), f32),
        'lru_w_i': nrm(ks[8], (DEPTH, 2, RNN_BLOCKS, RNN_BLOCK_W, RNN_BLOCK_W), RNN_BLOCK_W),
        'lru_b_i': 0.01 * jax.random.normal(ks[10], (DEPTH, 2, D_RNN), f32),
        'lru_lambda': lam,
        'w_rnn_proj': nrm(ks[11], (DEPTH, D_RNN, D_MODEL), D_RNN),
        'w_attn_proj': nrm(ks[12], (DEPTH, ATTN_OUT_W, D_MODEL), ATTN_OUT_W),
        'w_out': nrm(ks[13], (DEPTH, D_MODEL, D_MODEL), D_MODEL),
        'mlp_norm_g': 1.0 + 0.02 * jax.random.normal(ks[14], (DEPTH, D_MODEL), f32),
        'w_up': nrm(ks[15], (DEPTH, D_MODEL, D_FF), D_MODEL),
        'w_down': nrm(ks[16], (DEPTH, D_FF, D_MODEL), D_FF),
        'final_norm_g': 1.0 + 0.02 * jax.random.normal(ks[17], (D_MODEL,), f32),
    }


def reference(x_prompt, x_sample, mix_norm_g, w_in, conv_w, conv_b, lru_w_a, lru_b_a,
              lru_w_i, lru_b_i, lru_lambda, w_rnn_proj, w_attn_proj, w_out,
              mlp_norm_g, w_up, w_down, final_norm_g):
    def run(x):
        for l in range(DEPTH):
            x = encoder_layer(x, mix_norm_g[l], w_in[l], conv_w[l], conv_b[l],
                              lru_w_a[l], lru_b_a[l], lru_w_i[l], lru_b_i[l], lru_lambda[l],
                              w_rnn_proj[l], w_attn_proj[l], w_out[l],
                              mlp_norm_g[l], w_up[l], w_down[l])
        return rmsnorm(x, final_norm_g)

    y_prompt = run(x_prompt)
    y_sample = run(x_sample)
    return (y_prompt, y_sample)
```

```python
import contextlib
import math

import numpy as np
import concourse.bass as bass
import concourse.mybir as mybir
from concourse.bass_utils import run_bass_kernel_spmd

F32 = mybir.dt.float32
BF16 = mybir.dt.bfloat16
AF = mybir.ActivationFunctionType
ALU = mybir.AluOpType

D = 2048
KC = 16
HALO = 1024
EPS = 1e-6
QSCALE = 1.0 / math.sqrt(128.0)
DEBUG_SCRATCH = False


class Op:
    __slots__ = ("eng", "fn", "deps", "has_dep", "tick", "is_dma", "dsem", "dtarget", "ndma", "name")

    def __init__(self, eng, fn, is_dma, ndma, name):
        self.eng = eng
        self.fn = fn
        self.deps = set()
        self.has_dep = False
        self.tick = 0
        self.is_dma = is_dma
        self.dsem = None
        self.dtarget = 0
        self.ndma = ndma
        self.name = name


class Prog:
    ENGS = ("pe", "act", "dve", "pool", "sp")
    NDSEM = 24

    def __init__(self, nc):
        self.nc = nc
        self.ops = {e: [] for e in self.ENGS}
        self.key_w = {}
        self.key_r = {}
        self.dma_n = {e: 0 for e in self.ENGS}
        self.dma_tot = {}

    def add(self, eng, fn, reads=(), writes=(), dma=False, ndma=1, name=""):
        op = Op(eng, fn, dma, ndma, name)
        deps = op.deps
        bank_r = tuple(k for k in reads if isinstance(k, tuple) and k[0] == "bank")
        if bank_r:
            reads = tuple(k for k in reads if k not in bank_r)
            writes = tuple(writes) + tuple(k for k in bank_r if k not in writes)
        reads = tuple(reads) + ("PHASE",)
        for k in reads:
            w = self.key_w.get(k)
            if w is not None:
                deps.add(w)
        for k in writes:
            w = self.key_w.get(k)
            if w is not None:
                deps.add(w)
            for r in self.key_r.get(k, ()):
                deps.add(r)
        deps.discard(op)
        for d in deps:
            d.has_dep = True
        for k in reads:
            lst = self.key_r.setdefault(k, [])
            if not dma:
                for i, r in enumerate(lst):
                    if (not r.is_dma) and r.eng == eng:
                        lst[i] = op
                        break
                else:
                    lst.append(op)
            else:
                lst.append(op)
        for k in writes:
            self.key_w[k] = op
            self.key_r[k] = []
        if dma:
            slot = self.dma_n[eng] % self.NDSEM
            self.dma_n[eng] += 1
            op.dsem = (eng, slot)
            prev = self.dma_tot.get((eng, slot), 0)
            op.dtarget = prev + 16 * ndma
            self.dma_tot[(eng, slot)] = op.dtarget
        self.ops[eng].append(op)
        return op

    def dma(self, eng, out, in_, reads=(), writes=(), name=""):
        return self.add(eng, lambda e: [e.dma_start(out=out, in_=in_)], reads, writes, dma=True, ndma=1, name=name)

    def barrier(self, tiny):
        self.add("pool", lambda e: e.memset(tiny, 0.0), reads=(), writes=("PHASE",), name="barrier")

    def emit(self):
        nc = self.nc
        for e in self.ENGS:
            t = 0
            for op in self.ops[e]:
                if (not op.is_dma) and op.has_dep:
                    t += 1
                    op.tick = t
        with contextlib.ExitStack() as st:
            esem = {e: st.enter_context(nc.semaphore("s_" + e)) for e in ("pe", "act", "dve", "pool")}
            dsem = {}
            for e in ("sp", "act", "pool"):
                for s in range(min(self.NDSEM, self.dma_n[e])):
                    dsem[(e, s)] = st.enter_context(nc.semaphore("d_%s_%d" % (e, s)))
            block = st.enter_context(nc.Block())

            def run(ename):
                def body(eng):
                    waited = {}
                    for op in self.ops[ename]:
                        need = {}
                        for d in op.deps:
                            if d.is_dma:
                                k, v = d.dsem, d.dtarget
                            else:
                                k, v = d.eng, d.tick
                            if need.get(k, 0) < v:
                                need[k] = v
                        if op.is_dma:
                            g = op.dtarget - 16 * op.ndma
                            if g > 0 and need.get(op.dsem, 0) < g:
                                need[op.dsem] = g
                        for k, v in need.items():
                            if waited.get(k, 0) >= v:
                                continue
                            waited[k] = v
                            eng.wait_ge(dsem[k] if isinstance(k, tuple) else esem[k], v)
                        r = op.fn(eng)
                        if r is None:
                            continue
                        if op.is_dma:
                            assert len(r) == op.ndma, (op.name, len(r), op.ndma)
                            for ins in r:
                                ins.then_inc(dsem[op.dsem], 16)
                        elif op.has_dep:
                            ins = r[-1] if isinstance(r, (list, tuple)) else r
                            ins.then_inc(esem[ename], 1)

                return body

            block.tensor(run("pe"))
            block.scalar(run("act"))
            block.vector(run("dve"))
            block.gpsimd(run("pool"))
            block.sync(run("sp"))


class Arena:
    def __init__(self, t, words):
        self.t = t
        self.words = words
        self.off = 0
        self.base = 0

    def set_base(self):
        self.base = self.off

    def reset(self):
        self.off = self.base

    def f32(self, n):
        assert self.off + n <= self.words, ("arena overflow", self.off + n)
        ap = self.t[:, self.off:self.off + n]
        self.off += n
        return ap

    def bf16(self, n):
        w = (n + 1) // 2
        assert self.off + w <= self.words, ("arena overflow", self.off + w)
        ap = self.t[:, self.off:self.off + w].bitcast(BF16)
        self.off += w
        return ap


def ss(start, n, step):
    return slice(start, start + (n - 1) * step + 1, step)


def op_act(P, out, in_, func, reads, writes, scale=1.0, bias=0.0, accum=None):
    kw = dict(out=out, in_=in_, func=func, scale=scale, bias=bias)
    if accum is not None:
        kw["accum_out"] = accum
    return P.add("act", lambda e: e.activation(**kw), reads, writes)


def op_tt(P, eng, out, in0, in1, op, reads, writes):
    return P.add(eng, lambda e: e.tensor_tensor(out=out, in0=in0, in1=in1, op=op), reads, writes)


def op_ts(P, eng, out, in0, s1, s2, op0, op1, reads, writes):
    if s2 is None:
        return P.add(eng, lambda e: e.tensor_scalar(out=out, in0=in0, scalar1=s1, scalar2=None, op0=op0), reads, writes)
    return P.add(eng, lambda e: e.tensor_scalar(out=out, in0=in0, scalar1=s1, scalar2=s2, op0=op0, op1=op1), reads, writes)


def op_stt(P, out, in0, scalar, in1, op0, op1, reads, writes):
    return P.add("dve", lambda e: e.scalar_tensor_tensor(out=out, in0=in0, scalar=scalar, in1=in1, op0=op0, op1=op1), reads, writes)


def op_copy(P, eng, out, in_, reads, writes):
    if eng == "act":
        return op_act(P, out, in_, AF.Copy, reads, writes)
    return P.add(eng, lambda e: e.tensor_copy(out=out, in_=in_), reads, writes)


def op_mm(P, out, pairs, reads, writes):
    def fn(e):
        n = len(pairs)
        ins = None
        for i, (l, r) in enumerate(pairs):
            ins = e.matmul(out, lhsT=l, rhs=r, start=(i == 0), stop=(i == n - 1))
        return ins

    return P.add("pe", fn, reads, writes)


def op_tr(P, items, ident, reads, writes):
    def fn(e):
        ins = None
        for (o, i) in items:
            ins = e.transpose(o, i, ident)
        return ins

    return P.add("pe", fn, reads, writes)


def build_program(T):
    NTO = (T + 2 * HALO) // 512
    NTT = T // 512 + 1
    NOWN = T // 512
    TO = NTO * 512
    TTK = NTT * 512

    nc = bass.Bass("TRN2", target_bir_lowering=False)

    def din(name, shape, dt=F32):
        return nc.dram_tensor(name, list(shape), dt, kind="ExternalInput").ap()

    def dscr(name, shape, dt):
        kind = "ExternalOutput" if DEBUG_SCRATCH else "Internal"
        return nc.dram_tensor(name, list(shape), dt, kind=kind).ap()

    xo = din("xo", [TO, D])
    xt = din("xt", [TTK, D])
    y = nc.dram_tensor("y", [T, D], F32, kind="ExternalOutput").ap()
    w_in = din("w_in", [D, 12800])
    w_rnn = din("w_rnn", [D, D])
    w_att = din("w_att", [512, D])
    w_out = din("w_out", [D, D])
    w_up = din("w_up", [D, 8192])
    w_dn = din("w_dn", [8192, D])
    wg_src = din("wg", [2, 3, 16, 128, 128])
    prm = din("prm", [128, 23 * 16])
    gvec = din("gvec", [128, 2 * 16])
    gfin = din("gfin", [1, D])
    cst = din("cst", [TO, 128])
    msk = din("msk", [3, 128, 256])
    flg = din("flg", [128, 2])

    XN = dscr("XN", [16, 128, TO], BF16)
    XNo = dscr("XNo", [16, 128, TTK], BF16)
    XR = dscr("XR", [16, 128, T + 1024], F32)
    XRo = dscr("XRo", [16, 128, TTK], F32)
    GG = dscr("GG", [16, 128, T], F32)
    HG = dscr("HG", [16, 128, T], BF16)
    QT = dscr("QT", [12, 128, T], BF16)
    KT = dscr("KT", [12, 128, TO], BF16)
    VV = dscr("VV", [TO, 1536], BF16)
    AT = dscr("AT", [4, 128, T], BF16)
    WIG = dscr("WIG", [32, 128, 16, 128], BF16)
    WRN = dscr("WRN", [16, 128, 16, 128], BF16)
    WAT = dscr("WAT", [16, 128, 4, 128], BF16)
    WOU = dscr("WOU", [4, 128, 16, 512], BF16)
    WUP = dscr("WUP", [16, 128, 16, 512], BF16)
    WDN = dscr("WDN", [4, 128, 64, 512], BF16)

    AW = 52224
    with contextlib.ExitStack() as st:
        arena_t = st.enter_context(nc.sbuf_tensor("arena", [128, AW], F32))
        banks = [st.enter_context(nc.psum_tensor("pb%d" % i, [128, 512], F32)) for i in range(8)]
        A = Arena(arena_t, AW)
        P = Prog(nc)

        def bank(i):
            return banks[i][:]

        def bankb(i):
            return banks[i][:].bitcast(BF16)

        ident32 = A.f32(128)
        ident = A.bf16(128)
        ones = A.bf16(128)
        mask32 = A.f32(3 * 256)
        maskb = A.bf16(3 * 256)
        tiny = A.f32(4)
        flags = A.f32(2)
        soth = A.f32(16)
        initf = A.f32(16)
        initb = A.f32(16)
        gv = A.f32(32)
        A.set_base()

        def mk_ident(e):
            e.memset(ident32, 1.0)
            return e.affine_select(out=ident32, in_=ident32, pattern=[[-1, 128]], compare_op=ALU.is_equal,
                                   fill=0.0, base=0, channel_multiplier=1)

        P.add("pool", mk_ident, writes=["ident32"])
        op_copy(P, "pool", ident, ident32, ["ident32"], ["ident"])
        P.add("pool", lambda e: e.memset(ones, 1.0), writes=["ones"])
        P.dma("sp", mask32.rearrange("p (m q) -> p m q", m=3), msk.rearrange("m p q -> p m q"), writes=["mask32"])
        op_copy(P, "pool", maskb, mask32, ["mask32"], ["maskb"])
        P.dma("sp", flags, flg, writes=["flags"])
        P.dma("sp", gv, gvec, writes=["gv"])
        P.add("pool", lambda e: e.memset(soth, 0.0), writes=["soth"])

        def phase_p1a():
            A.reset()
            gfull = A.f32(16 * 128).rearrange("p (k c) -> p k c", k=16)
            xs32 = [A.f32(D) for _ in range(8)]
            junk = A.bf16(D)
            xsb = [A.bf16(D) for _ in range(2)]
            xnTs = [A.bf16(16 * 512).rearrange("p (k t) -> p k t", k=16) for _ in range(2)]
            ssq = [A.f32(4) for _ in range(2)]
            ms = [A.f32(4) for _ in range(2)]
            rstd = [A.f32(4) for _ in range(2)]
            for k in range(16):
                op_copy(P, "pool", gfull[:, k, :], gv[:, k:k + 1].to_broadcast([128, 128]), ["gv"], [("gfull", k)])
            gkeys = [("gfull", k) for k in range(16)]
            cnt = 0
            for (src, ntile, dst, dname) in ((xo, NTO, XN, "XN"), (xt, NTT, XNo, "XNo")):
                for i in range(ntile):
                    par = cnt % 2
                    cnt += 1
                    for s in range(4):
                        b = xs32[par * 4 + s]
                        P.dma("sp", b, src[(i * 4 + s) * 128:(i * 4 + s + 1) * 128, :], writes=[("xs32", par, s)])
                        op_act(P, junk, b, AF.Square, [("xs32", par, s)], ["junk", ("ssq", par, s)],
                               accum=ssq[par][:, s:s + 1])
                    op_ts(P, "dve", ms[par], ssq[par], 1.0 / D, EPS, ALU.mult, ALU.add,
                          [("ssq", par, s) for s in range(4)], [("ms", par)])
                    op_act(P, ms[par], ms[par], AF.Sqrt, [("ms", par)], [("ms", par)])
                    P.add("dve", (lambda o, i_: (lambda e: e.reciprocal(out=o, in_=i_)))(rstd[par], ms[par]),
                          [("ms", par)], [("rstd", par)])
                    xn = xnTs[par]
                    for s in range(4):
                        sb_ = xsb[s % 2]
                        op_ts(P, "dve", sb_, xs32[par * 4 + s], rstd[par][:, s:s + 1], None, ALU.mult, None,
                              [("xs32", par, s), ("rstd", par)], [("xsb", s % 2)])
                        for h in range(2):
                            bk = 4 + h
                            op_tr(P, [(bankb(bk)[:, j * 128:(j + 1) * 128], sb_[:, (h * 8 + j) * 128:(h * 8 + j + 1) * 128])
                                      for j in range(8)], ident, [("xsb", s % 2), "ident"], [("bank", bk)])
                            op_tt(P, "dve", xn[:, h * 8:(h + 1) * 8, s * 128:(s + 1) * 128],
                                  bankb(bk).rearrange("p (k c) -> p k c", k=8), gfull[:, h * 8:(h + 1) * 8, :], ALU.mult,
                                  [("bank", bk)] + gkeys, [("xnT", par, s, h)])
                    P.dma("pool", dst[:, :, i * 512:(i + 1) * 512].rearrange("k p t -> p k t"), xn,
                          reads=[("xnT", par, s, h) for s in range(4) for h in range(2)], writes=[(dname, i)])

        def load_resident(Wres, col0, ncols, stage, tag):
            i = 0
            engs = ("act", "dve", "pool")
            for cb in range(ncols // 512):
                for kh in range(2):
                    sg = stage[i % 2].rearrange("p (k c) -> p k c", k=8)
                    P.dma("sp", sg, w_in[kh * 1024:(kh + 1) * 1024, col0 + cb * 512: col0 + (cb + 1) * 512]
                          .rearrange("(k p) c -> p k c", p=128), writes=[("wstage", i % 2)])
                    op_copy(P, engs[i % 3], Wres[:, kh * 8:(kh + 1) * 8, cb * 512:(cb + 1) * 512], sg,
                            [("wstage", i % 2)], [("wres", tag, cb, kh)])
                    i += 1
            return [("wres", tag, cb, kh) for cb in range(ncols // 512) for kh in range(2)]

        def phase_g_formA(tag, col0, tiles_list, evac):
            A.reset()
            Wres = A.bf16(16 * 2048).rearrange("p (k c) -> p k c", k=16)
            stage = [A.f32(8 * 512) for _ in range(2)]
            xb = [A.bf16(16 * 512).rearrange("p (k t) -> p k t", k=16) for _ in range(2)]
            wkeys = load_resident(Wres, col0, 2048, stage, tag)
            state = evac(None, None, None, None, init=True)
            wcg = make_wc() if tag == "xr" else None
            bk = 0

            def load_tile(ti):
                srcT, sname, i = tiles_list[ti]
                P.dma("sp", xb[ti % 2], srcT[:, :, i * 512:(i + 1) * 512].rearrange("k p t -> p k t"),
                      reads=[(sname, i)], writes=[("xb", ti % 2)])

            load_tile(0)
            for ti, (srcT, sname, i) in enumerate(tiles_list):
                x_ = xb[ti % 2]
                for oc in range(16):
                    b = bk % 4
                    bk += 1
                    op_mm(P, bank(b), [(Wres[:, k, oc * 128:(oc + 1) * 128], x_[:, k, :]) for k in range(16)],
                          [("xb", ti % 2)] + wkeys, [("bank", b)])
                    evac(b, oc, ti, (sname, i), state=state)
                    if oc == 0 and ti + 1 < len(tiles_list):
                        load_tile(ti + 1)
                    if wcg is not None and oc in (2, 7, 12):
                        next(wcg, None)
            if wcg is not None:
                for _ in wcg:
                    pass

        def make_evac_xr():
            def evac(b, oc, ti, info, init=False, state=None):
                if init:
                    return dict(stg=[A.f32(4 * 512).rearrange("p (k t) -> p k t", k=4) for _ in range(2)], n=0)
                sname, i = info
                g = state["n"] // 4
                stg = state["stg"][g % 2]
                j = oc % 4
                op_copy(P, "act" if oc % 2 == 0 else "dve", stg[:, j, :], bank(b), [("bank", b)], [("stg", g % 2, j)])
                state["n"] += 1
                if j == 3:
                    if sname == "XN":
                        dst = XR[oc - 3:oc + 1, :, (i - 1) * 512:i * 512]
                        wk = ("XR", oc // 4, i)
                    else:
                        dst = XRo[oc - 3:oc + 1, :, i * 512:(i + 1) * 512]
                        wk = ("XRo", oc // 4, i)
                    P.dma("pool", dst.rearrange("k p t -> p k t"), stg, reads=[("stg", g % 2, jj) for jj in range(4)],
                          writes=[wk])
            return evac

        def make_evac_gelu():
            def evac(b, oc, ti, info, init=False, state=None):
                if init:
                    return dict(stg=[A.f32(4 * 512).rearrange("p (k t) -> p k t", k=4) for _ in range(2)], n=0,
                                t1=[A.f32(512) for _ in range(3)], t2=[A.f32(512) for _ in range(3)])
                sname, i = info
                n = state["n"]
                g = n // 4
                stg = state["stg"][g % 2]
                j = oc % 4
                r3 = n % 3
                t1 = state["t1"][r3]
                t2 = state["t2"][r3]
                op_act(P, t1, bank(b), AF.Square, [("bank", b)], [("t1", r3)])
                op_act(P, t2, bank(b), AF.Copy, [("bank", b)], [("t2", r3)], scale=0.5)
                op_ts(P, "dve", t1, t1, 0.044715, 1.0, ALU.mult, ALU.add, [("t1", r3)], [("t1", r3)])
                op_tt(P, "pool", t1, t1, t2, ALU.mult, [("t1", r3), ("t2", r3)], [("t1", r3)])
                op_act(P, t1, t1, AF.Tanh, [("t1", r3)], [("t1", r3)], scale=1.5957691216057308)
                op_stt(P, stg[:, j, :], t1, 1.0, t2, ALU.add, ALU.mult, [("t1", r3), ("t2", r3)], [("stg", g % 2, j)])
                state["n"] += 1
                if j == 3:
                    P.dma("pool", GG[oc - 3:oc + 1, :, (i - 2) * 512:(i - 1) * 512].rearrange("k p t -> p k t"), stg,
                          reads=[("stg", g % 2, jj) for jj in range(4)], writes=[("GG", oc // 4, i - 2)])
            return evac

        def phase_g_formB(tag, col0, tiles, kind):
            A.reset()
            Wres = A.bf16(16 * 1536).rearrange("p (k c) -> p k c", k=16)
            stage = [A.f32(8 * 512) for _ in range(2)]
            xb = [A.bf16(16 * 512).rearrange("p (k t) -> p k t", k=16) for _ in range(2)]
            wkeys = load_resident(Wres, col0, 1536, stage, tag)
            if kind in ("q", "k"):
                cs = [A.f32(4 * 128).rearrange("p (s x) -> p s x", s=4) for _ in range(2)]
                qb = [[A.bf16(512) for _ in range(12)] for _ in range(2)]
                rt = [[A.f32(64).rearrange("p (h e) -> p h e", h=4) for _ in range(4)] for _ in range(2)]
                outs = [A.bf16(12 * 512).rearrange("p (h t) -> p h t", h=12) for _ in range(2)]
            else:
                outs = [A.bf16(4 * 1536).rearrange("p (s c) -> p s c", s=4) for _ in range(2)]
            bk = 0
            n = 0
            for ti, i in enumerate(tiles):
                x_ = xb[ti % 2]
                P.dma("sp", x_, XN[:, :, i * 512:(i + 1) * 512].rearrange("k p t -> p k t"),
                      reads=[("XN", i)], writes=[("xb", ti % 2)])
                ot = outs[ti % 2]
                okeys = []
                if kind in ("q", "k"):
                    c_ = cs[ti % 2]
                    P.dma("sp", c_, cst[i * 512:(i + 1) * 512, :].rearrange("(s p) x -> p s x", p=128), writes=[("cs", ti % 2)])
                    if kind == "q":
                        cf = c_.rearrange("p s x -> p (s x)")
                        op_ts(P, "pool", cf, cf, QSCALE, None, ALU.mult, None, [("cs", ti % 2)], [("cs", ti % 2)])
                for cb in range(3):
                    for s in range(4):
                        b = bk % 4
                        bk += 1
                        op_mm(P, bank(b), [(x_[:, k, s * 128:(s + 1) * 128], Wres[:, k, cb * 512:(cb + 1) * 512]) for k in range(16)],
                              [("xb", ti % 2)] + wkeys, [("bank", b)])
                        if kind == "v":
                            kk = ("outs", ti % 2, cb, s)
                            op_copy(P, "act" if n % 2 == 0 else "dve", ot[:, s, cb * 512:(cb + 1) * 512], bank(b), [("bank", b)], [kk])
                            okeys.append(kk)
                            n += 1
                            continue
                        r2 = n % 2
                        n += 1
                        qi = cb * 4 + s
                        qb_ = qb[ti % 2][qi]
                        qk = ("qb", ti % 2, qi)
                        op_act(P, qb_, bank(b), AF.Copy, [("bank", b)], [qk], scale=(QSCALE if kind == "q" else 1.0))
                        p4 = bank(b).rearrange("p (h e) -> p h e", h=4)
                        qb4 = qb_.rearrange("p (h e) -> p h e", h=4)
                        cos4 = c_[:, s, 0:64].rearrange("p (h e) -> p h e", h=4)
                        sin4 = c_[:, s, 64:128].rearrange("p (h e) -> p h e", h=4)
                        t1 = p4[:, :, 0:16]
                        t2 = p4[:, :, 16:32]
                        ra, rb, rc, rd = rt[r2]
                        rk = [("rt", r2, z) for z in range(4)]
                        op_tt(P, "dve", ra, t1, cos4, ALU.mult, [("bank", b), ("cs", ti % 2)], [rk[0]])
                        op_tt(P, "dve", rb, t2, sin4, ALU.mult, [("bank", b), ("cs", ti % 2)], [rk[1]])
                        op_tt(P, "dve", rc, t2, cos4, ALU.mult, [("bank", b), ("cs", ti % 2)], [rk[2]])
                        op_tt(P, "dve", rd, t1, sin4, ALU.mult, [("bank", b), ("cs", ti % 2)], [rk[3]])
                        op_tt(P, "dve", qb4[:, :, 0:16], ra, rb, ALU.subtract, [rk[0], rk[1], qk], [qk])
                        op_tt(P, "dve", qb4[:, :, 16:32], rc, rd, ALU.add, [rk[2], rk[3], qk], [qk])
                if kind in ("q", "k"):
                    for g6 in range(6):
                        tb = 4 + (g6 % 2)
                        items = []
                        rkeys = ["ident"]
                        for u in range(2):
                            qi = g6 * 2 + u
                            rkeys.append(("qb", ti % 2, qi))
                            for h in range(4):
                                items.append((bankb(tb)[:, (u * 4 + h) * 128:(u * 4 + h + 1) * 128],
                                              qb[ti % 2][qi][:, h * 128:(h + 1) * 128]))
                        op_tr(P, items, ident, rkeys, [("bank", tb)])
                        for u in range(2):
                            qi = g6 * 2 + u
                            cb, s = qi // 4, qi % 4
                            kk = ("outs", ti % 2, cb, s)
                            op_copy(P, "act" if u == 0 else "dve", ot[:, cb * 4:(cb + 1) * 4, s * 128:(s + 1) * 128],
                                    bankb(tb)[:, u * 512:(u + 1) * 512].rearrange("p (h t) -> p h t", h=4), [("bank", tb)], [kk])
                            okeys.append(kk)
                if kind == "q":
                    P.dma("pool", QT[:, :, (i - 2) * 512:(i - 1) * 512].rearrange("h p t -> p h t"), ot, reads=okeys, writes=[("QT", i - 2)])
                elif kind == "k":
                    P.dma("pool", KT[:, :, i * 512:(i + 1) * 512].rearrange("h p t -> p h t"), ot, reads=okeys, writes=[("KT", i)])
                else:
                    P.dma("pool", VV[i * 512:(i + 1) * 512, :].rearrange("(s p) c -> p s c", p=128), ot, reads=okeys, writes=[("VV", i)])

        def make_wc():
            NS = 2
            stage = [A.f32(8 * 512) for _ in range(NS)]
            bst = [A.bf16(8 * 512) for _ in range(NS)]
            engs = ("pool", "act", "dve")
            cnt = [0]

            def conv(src, K, c0, ncols, dst, PW, dname):
                kcs = K // 128
                for cb in range(ncols // 512):
                    for kh in range((kcs + 7) // 8):
                        nk = min(8, kcs - kh * 8)
                        r = cnt[0] % NS
                        cnt[0] += 1
                        sg = stage[r][:, 0:nk * 512].rearrange("p (k c) -> p k c", k=nk)
                        P.dma("sp", sg, src[kh * 1024:kh * 1024 + nk * 128, c0 + cb * 512:c0 + (cb + 1) * 512]
                              .rearrange("(k p) c -> p k c", p=128), writes=[("wcs", r)])
                        npn = 512 // PW
                        bo = bst[r][:, 0:nk * 512]
                        op_copy(P, engs[cnt[0] % 3], bo.rearrange("p (n k c) -> p k n c", n=npn, k=nk),
                                sg.rearrange("p k (n c) -> p k n c", n=npn), [("wcs", r)], [("wcb", r)])
                        P.dma("pool", dst[cb * npn:(cb + 1) * npn, :, kh * 8:kh * 8 + nk, :].rearrange("n p k c -> p n k c"),
                              bo.rearrange("p (n k c) -> p n k c", n=npn, k=nk), reads=[("wcb", r)], writes=[(dname, cb, kh)])
                        yield

            def gen():
                yield from conv(w_in, 2048, 8704, 4096, WIG, 128, "WIG")
                yield from conv(w_rnn, 2048, 0, 2048, WRN, 128, "WRN")
                yield from conv(w_att, 512, 0, 2048, WAT, 128, "WAT")
                yield from conv(w_out, 2048, 0, 2048, WOU, 512, "WOU")
                yield from conv(w_up, 2048, 0, 8192, WUP, 512, "WUP")
                yield from conv(w_dn, 8192, 0, 2048, WDN, 512, "WDN")
            return gen()

        def phase_p2():
            A.reset()
            NB = T // 1024
            pr = A.f32(23 * 16)
            prv = pr.rearrange("p (n c) -> p n c", n=23)
            hba = A.f32(48).rearrange("p (n c) -> p n c", n=3)
            hbi = A.f32(48).rearrange("p (n c) -> p n c", n=3)
            ncs = A.f32(48).rearrange("p (n c) -> p n c", n=3)
            hncs = A.f32(48).rearrange("p (n c) -> p n c", n=3)
            NSET = 4
            wgs = [A.f32(128 * 6).rearrange("p (g d o) -> p g d o", g=2, d=3) for _ in range(2)]
            wgbs = [A.bf16(128 * 6).rearrange("p (g d o) -> p g d o", g=2, d=3) for _ in range(2)]
            xr = A.f32(T + 8)
            xc = A.f32(T)
            xcb = A.bf16(T)
            hf = A.f32(T)
            dgs = [[A.f32(128) for _ in range(5)] for _ in range(2)]
            Ab = [A.f32(1024) for _ in range(NSET)]
            Bb = [A.f32(1024) for _ in range(NSET)]
            Cb = [A.f32(1024) for _ in range(NSET)]
            ggb = [A.f32(1024) for _ in range(2)]
            HBb = [A.f32(1024) for _ in range(2)]
            OUb = [A.bf16(1024) for _ in range(2)]

            P.dma("sp", pr, prm, writes=["pr"])
            op_ts(P, "dve", hba.rearrange("p n c -> p (n c)"), pr[:, 160:208], 0.5, None, ALU.mult, None, ["pr"], ["hba"])
            op_ts(P, "dve", hbi.rearrange("p n c -> p (n c)"), pr[:, 208:256], 0.5, None, ALU.mult, None, ["pr"], ["hbi"])
            ncf = ncs.rearrange("p n c -> p (n c)")
            op_act(P, ncf, pr[:, 256:304], AF.Exp, ["pr"], ["ncs"], scale=-1.0)
            op_act(P, ncf, ncf, AF.Ln, ["ncs"], ["ncs"], bias=1.0)
            op_ts(P, "dve", ncf, ncf, -8.0, None, ALU.mult, None, ["ncs"], ["ncs"])
            op_ts(P, "dve", hncs.rearrange("p n c -> p (n c)"), ncf, 0.5, None, ALU.mult, None, ["ncs"], ["hncs"])
            def load_gates(c):
                sl = c % 2
                P.dma("sp", wgs[sl], wg_src[:, :, c].rearrange("g d i o -> i g d o"), writes=[("wgs", sl)])
                op_copy(P, "pool", wgbs[sl], wgs[sl], [("wgs", sl)], [("wgb", sl)])
                return wgbs[sl], ("wgb", sl)

            pkeys = ["hba", "hbi", "ncs", "hncs", "pr"]
            bkc = [0]
            blk = [0]
            outc = [0]

            def run_dir(c, d, reverse, ntok, init_ap, init_keys, emit_out, wgb, wgk, conv_fn=None):
                nb = ntok // 1024
                order = list(range(nb - 1, -1, -1) if reverse else range(nb))
                prev = None
                if conv_fn is not None:
                    conv_fn(c, order[0])
                for p0 in range(0, nb, 2):
                    info = []
                    for bi in order[p0:p0 + 2]:
                        r = blk[0] % NSET
                        blk[0] += 1
                        Ax, Bx, Cx = Ab[r], Bb[r], Cb[r]
                        if conv_fn is not None:
                            nxt = order.index(bi) + 1
                            if nxt < nb:
                                conv_fn(c, order[nxt])
                        for tl in range(2):
                            t0 = bi * 1024 + tl * 512
                            ba = 4 + bkc[0] % 4
                            bb = 4 + (bkc[0] + 1) % 4
                            bkc[0] += 2
                            op_mm(P, bank(ba), [(wgb[:, 0, d, :], xcb[:, t0:t0 + 512])], [wgk, ("xcb", bi, tl)], [("bank", ba)])
                            op_mm(P, bank(bb), [(wgb[:, 1, d, :], xcb[:, t0:t0 + 512])], [wgk, ("xcb", bi, tl)], [("bank", bb)])
                            op_act(P, Ax[:, tl * 512:(tl + 1) * 512], bank(ba), AF.Tanh, [("bank", ba)] + pkeys, [("A", r, tl)],
                                   scale=0.5, bias=hba[:, d, c:c + 1])
                            op_act(P, Bx[:, tl * 512:(tl + 1) * 512], bank(bb), AF.Tanh, [("bank", bb)] + pkeys, [("B", r, tl)],
                                   scale=0.5, bias=hbi[:, d, c:c + 1])
                        ak = [("A", r, 0), ("A", r, 1)]
                        bkeys = [("B", r, 0), ("B", r, 1)]
                        if reverse:
                            op_act(P, Ax, Ax, AF.Exp, ak + pkeys, ak, scale=hncs[:, d, c:c + 1], bias=hncs[:, d, c:c + 1])
                            op_tt(P, "dve", Cx, Ax, Ax, ALU.mult, ak, [("C", r)])
                        else:
                            op_act(P, Cx, Ax, AF.Exp, ak + pkeys, [("C", r)], scale=ncs[:, d, c:c + 1], bias=ncs[:, d, c:c + 1])
                            op_act(P, Ax, Ax, AF.Exp, ak + pkeys, ak, scale=hncs[:, d, c:c + 1], bias=hncs[:, d, c:c + 1])
                        info.append((bi, r, Ax, Bx, Cx, ak, bkeys))
                    for (bi, r, Ax, Bx, Cx, ak, bkeys) in info:
                        op_act(P, Cx, Cx, AF.Sqrt, [("C", r)], [("C", r)], scale=-0.25, bias=0.25)
                    for (bi, r, Ax, Bx, Cx, ak, bkeys) in info:
                        op_stt(P, Bx, Bx, 1.0, xc[:, bi * 1024:(bi + 1) * 1024], ALU.add, ALU.mult,
                               bkeys + [("xc", bi, 0), ("xc", bi, 1)], bkeys)
                        op_tt(P, "dve", Bx, Bx, Cx, ALU.mult, bkeys + [("C", r)], bkeys)
                        if prev is None:
                            ini, inik = init_ap, init_keys
                        else:
                            ini, inik = prev
                        if not reverse:
                            o = hf[:, bi * 1024:(bi + 1) * 1024]
                            P.add("dve", (lambda o_, a_, b_, i_: (lambda e: e.tensor_tensor_scan(
                                out=o_, data0=a_, data1=b_, initial=i_, op0=ALU.mult, op1=ALU.add)))(o, Ax, Bx, ini),
                                ak + bkeys + inik, [("hf", bi)])
                            prev = (hf[:, (bi + 1) * 1024 - 1:(bi + 1) * 1024], [("hf", bi)])
                        else:
                            ro = outc[0] % 2
                            outc[0] += 1
                            hbuf = HBb[ro]
                            P.add("dve", (lambda o_, a_, b_, i_: (lambda e: e.tensor_tensor_scan(
                                out=o_, data0=a_, data1=b_, initial=i_, op0=ALU.mult, op1=ALU.add)))(
                                hbuf[:, ::-1], Ax[:, ::-1], Bx[:, ::-1], ini),
                                ak + bkeys + inik, [("hb", ro)])
                            prev = (hbuf[:, 0:1], [("hb", ro)])
                            emit_out(bi, r, ro, hbuf, Cx)

            def load_own(c):
                P.dma("sp", xr[:, 0:T + 4], XR[c, :, 510:510 + T + 4],
                      reads=[("XR", c // 4, i) for i in range(1, NTO - 1)], writes=["xr"])

            def load_oth(c):
                P.add("pool", lambda e: e.memset(xr[:, 0:2], 0.0), writes=["xr"])
                P.dma("sp", xr[:, 2:T + 6], XRo[c, :, 0:T + 4], reads=[("XRo", c // 4, i) for i in range(NTT)], writes=["xr"])

            cvb = [0]

            def make_diags(c, ntap, w0):
                par = c % 2
                for j in range(ntap):
                    op_ts(P, "dve", dgs[par][j], ident32, prv[:, w0 + j, c:c + 1], None, ALU.mult, None,
                          ["ident32", "pr"], [("dg", par, j)])

            def conv_blk(c, bi, ntap, w0):
                par = c % 2
                for tl in range(2):
                    o = bi * 1024 + tl * 512
                    cb_ = cvb[0] % 4
                    cvb[0] += 1
                    op_mm(P, bank(cb_), [(dgs[par][j], xr[:, o + j:o + j + 512]) for j in range(ntap)],
                          ["xr"] + [("dg", par, j) for j in range(ntap)], [("bank", cb_)])
                    op_ts(P, "dve", xc[:, o:o + 512], bank(cb_), prv[:, 4, c:c + 1], None, ALU.add, None,
                          [("bank", cb_), "pr"], [("xc", bi, tl)])
                    op_act(P, xcb[:, o:o + 512], xc[:, o:o + 512], AF.Copy, [("xc", bi, tl)], [("xcb", bi, tl)])

            def conv_own(c, bi):
                conv_blk(c, bi, 4, 0)

            def conv_oth(c, bi):
                conv_blk(c, bi, 5, 5)

            for c in range(16):
                wgb_, wgk_ = load_gates(c)
                make_diags(c, 5, 5)
                load_oth(c)
                run_dir(c, 2, False, T, 0.0, [], None, wgb_, wgk_, conv_fn=conv_oth)
                op_copy(P, "dve", soth[:, c:c + 1], hf[:, T - 1:T], [("hf", NB - 1)], ["soth"])
            op_ts(P, "dve", initf, soth, flags[:, 0:1], None, ALU.mult, None, ["soth", "flags"], ["initf"])
            op_ts(P, "dve", initb, soth, flags[:, 1:2], None, ALU.mult, None, ["soth", "flags"], ["initb"])

            for c in range(16):
                wgb_, wgk_ = load_gates(c)
                make_diags(c, 4, 0)
                load_own(c)
                run_dir(c, 0, False, T, initf[:, c:c + 1], ["initf"], None, wgb_, wgk_, conv_fn=conv_own)

                def emit_out(bi, r, ro, hbuf, Cx, c=c):
                    g_ = ggb[ro]
                    P.dma("sp", g_, GG[c, :, bi * 1024:(bi + 1) * 1024],
                          reads=[("GG", c // 4, i) for i in range(bi * 2, bi * 2 + 2)], writes=[("gg", ro)])
                    op_tt(P, "dve", Cx, hbuf, hf[:, bi * 1024:(bi + 1) * 1024], ALU.add, [("hb", ro), ("hf", bi), ("C", r)], [("C", r)])
                    op_tt(P, "pool", OUb[ro], Cx, g_, ALU.mult, [("C", r), ("gg", ro)], [("ou", ro)])
                    P.dma("pool", HG[c, :, bi * 1024:(bi + 1) * 1024], OUb[ro], reads=[("ou", ro)], writes=[("HG", c, bi)])

                run_dir(c, 1, True, T, initb[:, c:c + 1], ["initb"], emit_out, wgb_, wgk_)

        def phase_p3():
            A.reset()
            NST = T // 2048
            Uacc = A.f32(4 * 2048).rearrange("p (j t) -> p j t", j=4)
            Dacc = A.f32(4 * 2048).rearrange("p (j t) -> p j t", j=4)
            qbuf = A.bf16(4 * 2048).rearrange("p (j t) -> p j t", j=4)
            kbuf = A.bf16(4 * 4096).rearrange("p (j t) -> p j t", j=4)
            vb = [A.bf16(512) for _ in range(3)]
            pt = [A.bf16(4 * 256).rearrange("p (j q) -> p j q", j=4) for _ in range(3)]
            atb = A.bf16(4 * 2048).rearrange("p (j t) -> p j t", j=4)
            mk = maskb.rearrange("p (m q) -> p m q", m=3)
            vcnt = [0]
            scnt = [0]
            for stile in range(NST):
                for g, d in enumerate((1, 4, 16)):
                    W = 2048 + 128 * d
                    B0 = 1024 + stile * 2048 - 64 * d
                    P.dma("sp", qbuf, QT[4 * g:4 * g + 4, :, stile * 2048:(stile + 1) * 2048].rearrange("h p t -> p h t"),
                          reads=[("QT", i) for i in range(stile * 4, stile * 4 + 4)], writes=["qbuf"])
                    P.dma("sp", kbuf[:, :, 0:W], KT[4 * g:4 * g + 4, :, B0:B0 + W].rearrange("h p t -> p h t"),
                          reads=[("KT", i) for i in range(NTO)], writes=["kbuf"])
                    nq = 16 // d
                    for r in range(d):
                        slots = {}

                        def do_chunk(m, r=r, d=d, g=g, stile=stile, nq=nq, B0=B0):
                            sl = vcnt[0] % 3
                            vcnt[0] += 1
                            row0 = B0 + r + 128 * d * m
                            P.dma("sp", vb[sl], VV[ss(row0, 128, d), 512 * g:512 * (g + 1)],
                                  reads=[("VV", i) for i in range(NTO)], writes=[("vb", sl)])
                            lo = 128 if m == 0 else 0
                            hi = 128 if m == nq else 256
                            if stile == 0 and m == 0:
                                mi = 1
                            elif stile == NST - 1 and m == nq:
                                mi = 2
                            else:
                                mi = 0
                            ptile = pt[sl]
                            for j in range(4):
                                sb_ = 4 + (scnt[0] % 2)
                                scnt[0] += 1
                                kc0 = r + 128 * d * m
                                qc0 = r + 128 * d * (m - 1) + lo * d
                                nqq = hi - lo
                                op_mm(P, bank(sb_)[:, lo:hi],
                                      [(kbuf[:, j, ss(kc0, 128, d)], qbuf[:, j, ss(qc0, nqq, d)])],
                                      ["kbuf", "qbuf"], [("bank", sb_)])
                                op_act(P, ptile[:, j, lo:hi], bank(sb_)[:, lo:hi], AF.Exp, [("bank", sb_)], [("pt", sl, j)])
                                op_tt(P, "pool", ptile[:, j, lo:hi], ptile[:, j, lo:hi], mk[:, mi, lo:hi], ALU.mult,
                                      [("pt", sl, j), "maskb"], [("pt", sl, j)])
                            slots[m] = sl

                        do_chunk(0)
                        for i in range(nq):
                            do_chunk(i + 1)
                            s0, s1 = slots[i], slots[i + 1]
                            pr_ = [("pt", s0, j) for j in range(4)] + [("pt", s1, j) for j in range(4)]

                            def pv(e, s0=s0, s1=s1):
                                ins = None
                                for j in range(4):
                                    e.matmul(bank(6)[:, j * 128:(j + 1) * 128], lhsT=vb[s0][:, j * 128:(j + 1) * 128],
                                             rhs=pt[s0][:, j, 128:256], start=True, stop=False)
                                    ins = e.matmul(bank(6)[:, j * 128:(j + 1) * 128], lhsT=vb[s1][:, j * 128:(j + 1) * 128],
                                                   rhs=pt[s1][:, j, 0:128], start=False, stop=True)
                                return ins

                            def dn(e, s0=s0, s1=s1):
                                ins = None
                                for j in range(4):
                                    e.matmul(bank(7)[:, j * 128:(j + 1) * 128], lhsT=ones, rhs=pt[s0][:, j, 128:256],
                                             start=True, stop=False)
                                    ins = e.matmul(bank(7)[:, j * 128:(j + 1) * 128], lhsT=ones, rhs=pt[s1][:, j, 0:128],
                                                   start=False, stop=True)
                                return ins

                            P.add("pe", pv, pr_ + [("vb", s0), ("vb", s1)], [("bank", 6)])
                            P.add("pe", dn, pr_ + ["ones"], [("bank", 7)])
                            c0 = r + 128 * d * i
                            usl = Uacc[:, :, ss(c0, 128, d)]
                            dsl = Dacc[:, :, ss(c0, 128, d)]
                            b6 = bank(6).rearrange("p (j q) -> p j q", j=4)
                            b7 = bank(7).rearrange("p (j q) -> p j q", j=4)
                            if g == 0:
                                op_copy(P, "dve", usl, b6, [("bank", 6)], ["Uacc"])
                                op_copy(P, "act", dsl, b7, [("bank", 7)], ["Dacc"])
                            else:
                                op_tt(P, "dve", usl, usl, b6, ALU.add, [("bank", 6), "Uacc"], ["Uacc"])
                                op_tt(P, "dve", dsl, dsl, b7, ALU.add, [("bank", 7), "Dacc"], ["Dacc"])
                Df = Dacc.rearrange("p j t -> p (j t)")
                Uf = Uacc.rearrange("p j t -> p (j t)")
                P.add("dve", lambda e: e.reciprocal(out=Df, in_=Df), ["Dacc"], ["Dacc"])
                op_tt(P, "dve", atb.rearrange("p j t -> p (j t)"), Uf, Df, ALU.mult, ["Uacc", "Dacc"], ["atb"])
                P.dma("pool", AT[:, :, stile * 2048:(stile + 1) * 2048].rearrange("j p t -> p j t"), atb,
                      reads=["atb"], writes=[("AT", stile)])

        def phase_p4():
            A.reset()
            gF = A.f32(D)
            R1 = A.off
            xnT = A.bf16(16 * 512).rearrange("p (k t) -> p k t", k=16)
            hgT = A.bf16(16 * 512).rearrange("p (k t) -> p k t", k=16)
            atT = A.bf16(4 * 512).rearrange("p (k t) -> p k t", k=4)
            mxb = A.bf16(16 * 512).rearrange("p (k t) -> p k t", k=16)
            A.off = R1
            HT = A.bf16(64 * 512).rearrange("p (k t) -> p k t", k=64)
            H1 = A.f32(4 * D).rearrange("p (s c) -> p s c", s=4)
            h1nT = A.bf16(16 * 512).rearrange("p (k t) -> p k t", k=16)
            hsb = [A.bf16(D) for _ in range(1)]
            junk = A.bf16(D)
            wp = [A.bf16(16 * 512) for _ in range(3)]
            tA = [A.f32(512) for _ in range(2)]
            tB = [A.f32(512) for _ in range(2)]
            rl = [A.f32(512) for _ in range(2)]
            ssq = A.f32(4)
            ms = A.f32(4)
            rstd = A.f32(4)
            ssq2 = A.f32(4)
            ms2 = A.f32(4)
            rstd2 = A.f32(4)
            P.dma("sp", gF, gfin.partition_broadcast(128), writes=["gF"])
            htkeys = [("HT", c) for c in range(64)]
            wpc = [0]
            wqc = [0]
            bkc = [0]

            def load_panel(src_ap, nelem, rkeys):
                sl = wpc[0] % 3
                wpc[0] += 1
                dst = wp[sl][:, 0:nelem]
                P.dma("sp", dst, src_ap, reads=rkeys, writes=[("wp", sl, j) for j in range(4)])
                return sl

            for t in range(NOWN):
                c0 = t * 512
                P.dma("sp", xnT, XN[:, :, (t + 2) * 512:(t + 3) * 512].rearrange("k p t -> p k t"),
                      reads=[("XN", t + 2)], writes=["xnT"] + htkeys)
                P.dma("sp", hgT, HG[:, :, c0:c0 + 512].rearrange("k p t -> p k t"),
                      reads=[("HG", c, t // 2) for c in range(16)], writes=["hgT"] + htkeys)
                P.dma("sp", atT, AT[:, :, c0:c0 + 512].rearrange("k p t -> p k t"),
                      reads=[("AT", t // 4)], writes=["atT"] + htkeys)
                for s in range(4):
                    P.dma("sp", H1[:, s, :], xo[HALO + c0 + s * 128:HALO + c0 + (s + 1) * 128, :], writes=[("H1", s)])
                for c in range(16):
                    sl = wpc[0] % 3
                    wpc[0] += 1
                    wq_ = wp[sl]
                    w_r = wq_[:, 0:2048].rearrange("p (k c) -> p k c", k=16)
                    w_gr = wq_[:, 2048:4096].rearrange("p (k c) -> p k c", k=16)
                    w_ga = wq_[:, 4096:6144].rearrange("p (k c) -> p k c", k=16)
                    w_a = wq_[:, 6144:6656].rearrange("p (k c) -> p k c", k=4)
                    P.dma("sp", w_r, WRN[c], writes=[("wp", sl, 0)])
                    P.dma("sp", w_gr, WIG[c], writes=[("wp", sl, 1)])
                    P.dma("sp", w_ga, WIG[16 + c], writes=[("wp", sl, 2)])
                    P.dma("sp", w_a, WAT[c], writes=[("wp", sl, 3)])
                    r2 = c % 2
                    b0 = 4 * (c % 2)
                    op_mm(P, bank(b0), [(w_r[:, k, :], hgT[:, k, :]) for k in range(16)], [("wp", sl, 0), "hgT"], [("bank", b0)])
                    op_mm(P, bank(b0 + 1), [(w_gr[:, k, :], xnT[:, k, :]) for k in range(16)], [("wp", sl, 1), "xnT"], [("bank", b0 + 1)])
                    op_act(P, tA[r2], bank(b0 + 1), AF.Tanh, [("bank", b0 + 1)], [("tA", r2)], scale=0.5)
                    op_stt(P, tA[r2], tA[r2], 1.0, bank(b0), ALU.add, ALU.mult, [("tA", r2), ("bank", b0)], [("tA", r2)])
                    op_mm(P, bank(b0 + 2), [(w_a[:, k, :], atT[:, k, :]) for k in range(4)], [("wp", sl, 3), "atT"], [("bank", b0 + 2)])
                    op_mm(P, bank(b0 + 3), [(w_ga[:, k, :], xnT[:, k, :]) for k in range(16)], [("wp", sl, 2), "xnT"], [("bank", b0 + 3)])
                    op_act(P, tB[r2], bank(b0 + 3), AF.Tanh, [("bank", b0 + 3)], [("tB", r2)], scale=0.5)
                    op_stt(P, tB[r2], tB[r2], 1.0, bank(b0 + 2), ALU.add, ALU.mult, [("tB", r2), ("bank", b0 + 2)], [("tB", r2)])
                    op_tt(P, "pool", tA[r2], tA[r2], tB[r2], ALU.add, [("tA", r2), ("tB", r2)], [("tA", r2)])
                    op_act(P, mxb[:, c, :], tA[r2], AF.Copy, [("tA", r2)], [("mxb", c)], scale=0.5)
                mxkeys = [("mxb", c) for c in range(16)]
                for cb in range(4):
                    sl = load_panel(WOU[cb].rearrange("p k c -> p (k c)"), 16 * 512, [])
                    wv = wp[sl].rearrange("p (k c) -> p k c", k=16)
                    for s in range(4):
                        b = bkc[0] % 4
                        bkc[0] += 1
                        op_mm(P, bank(b), [(mxb[:, k, s * 128:(s + 1) * 128], wv[:, k, :]) for k in range(16)],
                              mxkeys + [("wp", sl, j) for j in range(4)], [("bank", b)])
                        op_tt(P, "dve", H1[:, s, cb * 512:(cb + 1) * 512], H1[:, s, cb * 512:(cb + 1) * 512], bank(b), ALU.add,
                              [("bank", b), ("H1", s)], [("H1", s)])
                for s in range(4):
                    op_act(P, junk, H1[:, s, :], AF.Square, [("H1", s)], ["junk", ("ssq", s)], accum=ssq[:, s:s + 1])
                op_ts(P, "dve", ms, ssq, 1.0 / D, EPS, ALU.mult, ALU.add, [("ssq", s) for s in range(4)], ["ms"])
                op_act(P, ms, ms, AF.Sqrt, ["ms"], ["ms"])
                P.add("dve", lambda e: e.reciprocal(out=rstd, in_=ms), ["ms"], ["rstd"])
                for s in range(4):
                    sb_ = hsb[0]
                    op_act(P, sb_, H1[:, s, :], AF.Copy, [("H1", s), "rstd"], [("hsb", 0)], scale=rstd[:, s:s + 1])
                    for h in range(2):
                        bk = 4 + h
                        op_tr(P, [(bankb(bk)[:, j * 128:(j + 1) * 128], sb_[:, (h * 8 + j) * 128:(h * 8 + j + 1) * 128])
                                  for j in range(8)], ident, [("hsb", 0), "ident"], [("bank", bk)])
                        for j in range(8):
                            k = h * 8 + j
                            if h == 0:
                                op_ts(P, "dve", h1nT[:, k, s * 128:(s + 1) * 128], bankb(bk)[:, j * 128:(j + 1) * 128],
                                      gv[:, 16 + k:17 + k], None, ALU.mult, None, [("bank", bk), "gv"], [("h1nT", s, h, j)])
                            else:
                                op_act(P, h1nT[:, k, s * 128:(s + 1) * 128], bankb(bk)[:, j * 128:(j + 1) * 128], AF.Copy,
                                       [("bank", bk), "gv"], [("h1nT", s, h, j)], scale=gv[:, 16 + k:17 + k])
                hnkeys = [("h1nT", s, h, j) for s in range(4) for h in range(2) for j in range(8)]
                for pn in range(16):
                    sl = load_panel(WUP[pn].rearrange("p k c -> p (k c)"), 16 * 512, [])
                    wv = wp[sl].rearrange("p (k c) -> p k c", k=16)
                    for oc in range(4):
                        c = pn * 4 + oc
                        b = bkc[0] % 4
                        bkc[0] += 1
                        r3 = c % 2
                        op_mm(P, bank(b), [(wv[:, k, oc * 128:(oc + 1) * 128], h1nT[:, k, :]) for k in range(16)],
                              hnkeys + [("wp", sl, j) for j in range(4)], [("bank", b)])
                        op_act(P, rl[r3], bank(b), AF.Relu, [("bank", b)], [("rl", r3)])
                        op_tt(P, "pool", HT[:, c, :], rl[r3], rl[r3], ALU.mult, [("rl", r3)], [("HT", c)])
                for cb in range(4):
                    sls = []
                    for qd in range(4):
                        sls.append(None)
                    for qd in range(4):
                        sl = load_panel(WDN[cb, :, qd * 16:(qd + 1) * 16, :].rearrange("p k c -> p (k c)"), 16 * 512, [])
                        wv = wp[sl].rearrange("p (k c) -> p k c", k=16)
                        for s in range(4):
                            def grp(e, wv=wv, s=s, qd=qd):
                                ins = None
                                for k in range(16):
                                    ins = e.matmul(bank(s), lhsT=HT[:, qd * 16 + k, s * 128:(s + 1) * 128], rhs=wv[:, k, :],
                                                   start=(qd == 0 and k == 0), stop=(qd == 3 and k == 15))
                                return ins
                            P.add("pe", grp, [("HT", qd * 16 + k) for k in range(16)] + [("wp", sl, j) for j in range(4)], [("bank", s)])
                    for s in range(4):
                        op_tt(P, "dve", H1[:, s, cb * 512:(cb + 1) * 512], H1[:, s, cb * 512:(cb + 1) * 512], bank(s), ALU.add,
                              [("bank", s), ("H1", s)], [("H1", s)])
                    bkc[0] = 0
                for s in range(4):
                    op_act(P, junk, H1[:, s, :], AF.Square, [("H1", s)], ["junk", ("ssq2", s)], accum=ssq2[:, s:s + 1])
                op_ts(P, "dve", ms2, ssq2, 1.0 / D, EPS, ALU.mult, ALU.add, [("ssq2", s) for s in range(4)], ["ms2"])
                op_act(P, ms2, ms2, AF.Sqrt, ["ms2"], ["ms2"])
                P.add("dve", lambda e: e.reciprocal(out=rstd2, in_=ms2), ["ms2"], ["rstd2"])
                for s in range(4):
                    op_stt(P, H1[:, s, :], H1[:, s, :], rstd2[:, s:s + 1], gF, ALU.mult, ALU.mult, [("H1", s), "rstd2", "gF"], [("H1", s)])
                    P.dma("pool", y[c0 + s * 128:c0 + (s + 1) * 128, :], H1[:, s, :], reads=[("H1", s), "OUT"])

        phase_p1a()
        P.barrier(tiny)
        phase_g_formA("xr", 0, [(XN, "XN", i) for i in range(1, NTO - 1)] + [(XNo, "XNo", i) for i in range(NTT)], make_evac_xr())
        P.barrier(tiny)
        phase_g_formA("gr", 2048, [(XN, "XN", i) for i in range(2, NTO - 2)], make_evac_gelu())
        P.barrier(tiny)
        phase_g_formB("q", 4096, list(range(2, NTO - 2)), "q")
        P.barrier(tiny)
        phase_g_formB("k", 4096 + 1536, list(range(NTO)), "k")
        P.barrier(tiny)
        phase_g_formB("v", 4096 + 3072, list(range(NTO)), "v")
        P.barrier(tiny)
        phase_p2()
        P.barrier(tiny)
        phase_p3()
        P.barrier(tiny)
        phase_p4()
        P.add("sp", lambda e: None, writes=["OUT"])
        P.emit()
    return nc


def _pc(v):
    return np.ascontiguousarray(np.asarray(v, np.float32).reshape(16, 128).T)


def _rope_table(pos):
    inv_freq = (np.float32(500000.0) ** (-np.arange(0, 32, 2, dtype=np.float32) / np.float32(32))).astype(np.float32)
    ang = (pos.astype(np.float32)[:, None] * inv_freq[None, :]).astype(np.float32)
    cos = np.cos(ang).astype(np.float32)
    sin = np.sin(ang).astype(np.float32)
    return np.concatenate([np.tile(cos, (1, 4)), np.tile(sin, (1, 4))], axis=1).astype(np.float32)


def _masks(left_real, right_real):
    kk = np.arange(128)[:, None]
    qq = np.arange(256)[None, :]
    band = ((kk >= qq - 128) & (kk <= qq)).astype(np.float32)
    m_l = band if left_real else band * (kk >= 64)
    m_r = band if right_real else band * (kk < 64)
    return np.stack([band, m_l, m_r]).astype(np.float32)


_NC_CACHE = {}


def run_layer(T, seqs, P):
    if T not in _NC_CACHE:
        _NC_CACHE[T] = build_program(T)
    nc = _NC_CACHE[T]
    conv_w = np.asarray(P["conv_w"][0], np.float32)
    conv_b = np.asarray(P["conv_b"][0], np.float32)
    w_a = np.asarray(P["lru_w_a"][0], np.float32)
    w_i = np.asarray(P["lru_w_i"][0], np.float32)
    b_a = np.asarray(P["lru_b_a"][0], np.float32)
    b_i = np.asarray(P["lru_b_i"][0], np.float32)
    lam = np.asarray(P["lru_lambda"][0], np.float32)
    zeros = np.zeros(D, np.float32)
    shared = dict(
        w_in=np.ascontiguousarray(P["w_in"][0], np.float32), w_rnn=np.ascontiguousarray(P["w_rnn_proj"][0], np.float32),
        w_att=np.ascontiguousarray(P["w_attn_proj"][0], np.float32), w_out=np.ascontiguousarray(P["w_out"][0], np.float32),
        w_up=np.ascontiguousarray(P["w_up"][0], np.float32), w_dn=np.ascontiguousarray(P["w_down"][0], np.float32),
        gvec=np.ascontiguousarray(np.concatenate([_pc(P["mix_norm_g"][0]), _pc(P["mlp_norm_g"][0])], axis=1)),
        gfin=np.ascontiguousarray(np.asarray(P["final_norm_g"], np.float32).reshape(1, D)),
    )
    NTT = T // 512 + 1
    in_maps = []
    for (xs, p0) in seqs:
        S = xs.shape[0]
        xo = np.zeros((T + 2 * HALO, D), np.float32)
        lo = max(0, p0 - HALO)
        hi = min(S, p0 + T + HALO)
        xo[lo - (p0 - HALO):hi - (p0 - HALO)] = xs[lo:hi]
        xt = np.zeros((NTT * 512, D), np.float32)
        ff = fb = 0.0
        od = 0
        w5 = [conv_w[0], conv_w[1], conv_w[2], conv_w[3], zeros]
        if p0 > 0:
            assert p0 == T
            xt[0:T] = xs[0:T]
            xt[T:T + 2] = xs[T:T + 2]
            ff = 1.0
            od = 0
        elif S > T:
            xt[0:T] = xs[T:2 * T][::-1]
            xt[T] = xs[T - 1]
            xt[T + 1] = xs[T - 2]
            fb = 1.0
            od = 1
            w5 = [zeros, conv_w[3], conv_w[2], conv_w[1], conv_w[0]]
        wg = np.stack([np.stack([w_a[0], w_a[1], w_a[od]]), np.stack([w_i[0], w_i[1], w_i[od]])]).astype(np.float32)
        cols = [_pc(conv_w[j]) for j in range(4)] + [_pc(conv_b)] + [_pc(w) for w in w5]
        cols += [_pc(b_a[0]), _pc(b_a[1]), _pc(b_a[od])] + [_pc(b_i[0]), _pc(b_i[1]), _pc(b_i[od])]
        cols += [_pc(lam[0]), _pc(lam[1]), _pc(lam[od])] + [_pc(zeros)] * 4
        prm = np.ascontiguousarray(np.concatenate(cols, axis=1))
        pos = np.arange(p0 - HALO, p0 + T + HALO)
        m = dict(shared)
        m.update(xo=xo, xt=xt, wg=np.ascontiguousarray(wg), prm=prm, cst=_rope_table(np.maximum(pos, 0)),
                 msk=_masks(p0 > 0, p0 + T < S), flg=np.ascontiguousarray(np.tile(np.array([[ff, fb]], np.float32), (128, 1))))
        in_maps.append(m)
    res = run_bass_kernel_spmd(nc, in_maps, core_ids=list(range(8)))
    return res


def kernel(x_prompt, x_sample, mix_norm_g, w_in, conv_w, conv_b, lru_w_a, lru_b_a, lru_w_i, lru_b_i, lru_lambda,
           w_rnn_proj, w_attn_proj, w_out, mlp_norm_g, w_up, w_down, final_norm_g):
    x_prompt = np.asarray(x_prompt, np.float32)
    x_sample = np.asarray(x_sample, np.float32)
    T = x_prompt.shape[1]
    assert x_prompt.shape[0] == 4 and x_sample.shape[0] == 2 and x_sample.shape[1] == 2 * T
    P = dict(mix_norm_g=np.asarray(mix_norm_g), w_in=np.asarray(w_in), conv_w=np.asarray(conv_w), conv_b=np.asarray(conv_b),
             lru_w_a=np.asarray(lru_w_a), lru_b_a=np.asarray(lru_b_a), lru_w_i=np.asarray(lru_w_i), lru_b_i=np.asarray(lru_b_i),
             lru_lambda=np.asarray(lru_lambda), w_rnn_proj=np.asarray(w_rnn_proj), w_attn_proj=np.asarray(w_attn_proj),
             w_out=np.asarray(w_out), mlp_norm_g=np.asarray(mlp_norm_g), w_up=np.asarray(w_up), w_down=np.asarray(w_down),
             final_norm_g=np.asarray(final_norm_g))
    seqs = [(x_prompt[b], 0) for b in range(4)]
    for b in range(2):
        seqs += [(x_sample[b], 0), (x_sample[b], T)]
    res = run_layer(T, seqs, P)
    outs = [np.asarray(r["y"], np.float32) for r in res.results]
    y_prompt = np.stack(outs[0:4])
    y_sample = np.stack([np.concatenate([outs[4], outs[5]]), np.concatenate([outs[6], outs[7]])])
    return (y_prompt, y_sample)
```

```python
import contextlib
import math

import numpy as np
import concourse.bass as bass
import concourse.mybir as mybir
from concourse.bass_utils import run_bass_kernel_spmd

F32 = mybir.dt.float32
BF16 = mybir.dt.bfloat16
AF = mybir.ActivationFunctionType
ALU = mybir.AluOpType

D = 2048
KC = 16
HALO = 1024
EPS = 1e-6
QSCALE = 1.0 / math.sqrt(128.0)
DEBUG_SCRATCH = False


class Op:
    __slots__ = ("eng", "fn", "deps", "has_dep", "tick", "is_dma", "dsem", "dtarget", "ndma", "name")

    def __init__(self, eng, fn, is_dma, ndma, name):
        self.eng = eng
        self.fn = fn
        self.deps = set()
        self.has_dep = False
        self.tick = 0
        self.is_dma = is_dma
        self.dsem = None
        self.dtarget = 0
        self.ndma = ndma
        self.name = name


class Prog:
    ENGS = ("pe", "act", "dve", "pool", "sp")
    NDSEM = 24

    def __init__(self, nc):
        self.nc = nc
        self.ops = {e: [] for e in self.ENGS}
        self.key_w = {}
        self.key_r = {}
        self.dma_n = {e: 0 for e in self.ENGS}
        self.dma_tot = {}

    def add(self, eng, fn, reads=(), writes=(), dma=False, ndma=1, name=""):
        op = Op(eng, fn, dma, ndma, name)
        deps = op.deps
        bank_r = tuple(k for k in reads if isinstance(k, tuple) and k[0] == "bank")
        if bank_r:
            reads = tuple(k for k in reads if k not in bank_r)
            writes = tuple(writes) + tuple(k for k in bank_r if k not in writes)
        reads = tuple(reads) + ("PHASE",)
        for k in reads:
            w = self.key_w.get(k)
            if w is not None:
                deps.add(w)
        for k in writes:
            w = self.key_w.get(k)
            if w is not None:
                deps.add(w)
            for r in self.key_r.get(k, ()):
                deps.add(r)
        deps.discard(op)
        for d in deps:
            d.has_dep = True
        for k in reads:
            lst = self.key_r.setdefault(k, [])
            if not dma:
                for i, r in enumerate(lst):
                    if (not r.is_dma) and r.eng == eng:
                        lst[i] = op
                        break
                else:
                    lst.append(op)
            else:
                lst.append(op)
        for k in writes:
            self.key_w[k] = op
            self.key_r[k] = []
        if dma:
            slot = self.dma_n[eng] % self.NDSEM
            self.dma_n[eng] += 1
            op.dsem = (eng, slot)
            prev = self.dma_tot.get((eng, slot), 0)
            op.dtarget = prev + 16 * ndma
            self.dma_tot[(eng, slot)] = op.dtarget
        self.ops[eng].append(op)
        return op

    def dma(self, eng, out, in_, reads=(), writes=(), name=""):
        return self.add(eng, lambda e: [e.dma_start(out=out, in_=in_)], reads, writes, dma=True, ndma=1, name=name)

    def barrier(self, tiny):
        self.add("pool", lambda e: e.memset(tiny, 0.0), reads=(), writes=("PHASE",), name="barrier")

    def emit(self):
        nc = self.nc
        for e in self.ENGS:
            t = 0
            for op in self.ops[e]:
                if (not op.is_dma) and op.has_dep:
                    t += 1
                    op.tick = t
        with contextlib.ExitStack() as st:
            esem = {e: st.enter_context(nc.semaphore("s_" + e)) for e in ("pe", "act", "dve", "pool")}
            dsem = {}
            for e in ("sp", "act", "pool"):
                for s in range(min(self.NDSEM, self.dma_n[e])):
                    dsem[(e, s)] = st.enter_context(nc.semaphore("d_%s_%d" % (e, s)))
            block = st.enter_context(nc.Block())

            def run(ename):
                def body(eng):
                    waited = {}
                    for op in self.ops[ename]:
                        need = {}
                        for d in op.deps:
                            if d.is_dma:
                                k, v = d.dsem, d.dtarget
                            else:
                                k, v = d.eng, d.tick
                            if need.get(k, 0) < v:
                                need[k] = v
                        if op.is_dma:
                            g = op.dtarget - 16 * op.ndma
                            if g > 0 and need.get(op.dsem, 0) < g:
                                need[op.dsem] = g
                        for k, v in need.items():
                            if waited.get(k, 0) >= v:
                                continue
                            waited[k] = v
                            eng.wait_ge(dsem[k] if isinstance(k, tuple) else esem[k], v)
                        r = op.fn(eng)
                        if r is None:
                            continue
                        if op.is_dma:
                            assert len(r) == op.ndma, (op.name, len(r), op.ndma)
                            for ins in r:
                                ins.then_inc(dsem[op.dsem], 16)
                        elif op.has_dep:
                            ins = r[-1] if isinstance(r, (list, tuple)) else r
                            ins.then_inc(esem[ename], 1)

                return body

            block.tensor(run("pe"))
            block.scalar(run("act"))
            block.vector(run("dve"))
            block.gpsimd(run("pool"))
            block.sync(run("sp"))


class Arena:
    def __init__(self, t, words):
        self.t = t
        self.words = words
        self.off = 0
        self.base = 0

    def set_base(self):
        self.base = self.off

    def reset(self):
        self.off = self.base

    def f32(self, n):
        assert self.off + n <= self.words, ("arena overflow", self.off + n)
        ap = self.t[:, self.off:self.off + n]
        self.off += n
        return ap

    def bf16(self, n):
        w = (n + 1) // 2
        assert self.off + w <= self.words, ("arena overflow", self.off + w)
        ap = self.t[:, self.off:self.off + w].bitcast(BF16)
        self.off += w
        return ap


def ss(start, n, step):
    return slice(start, start + (n - 1) * step + 1, step)


def op_act(P, out, in_, func, reads, writes, scale=1.0, bias=0.0, accum=None):
    kw = dict(out=out, in_=in_, func=func, scale=scale, bias=bias)
    if accum is not None:
        kw["accum_out"] = accum
    return P.add("act", lambda e: e.activation(**kw), reads, writes)


def op_tt(P, eng, out, in0, in1, op, reads, writes):
    return P.add(eng, lambda e: e.tensor_tensor(out=out, in0=in0, in1=in1, op=op), reads, writes)


def op_ts(P, eng, out, in0, s1, s2, op0, op1, reads, writes):
    if s2 is None:
        return P.add(eng, lambda e: e.tensor_scalar(out=out, in0=in0, scalar1=s1, scalar2=None, op0=op0), reads, writes)
    return P.add(eng, lambda e: e.tensor_scalar(out=out, in0=in0, scalar1=s1, scalar2=s2, op0=op0, op1=op1), reads, writes)


def op_stt(P, out, in0, scalar, in1, op0, op1, reads, writes):
    return P.add("dve", lambda e: e.scalar_tensor_tensor(out=out, in0=in0, scalar=scalar, in1=in1, op0=op0, op1=op1), reads, writes)


def op_copy(P, eng, out, in_, reads, writes):
    if eng == "act":
        return op_act(P, out, in_, AF.Copy, reads, writes)
    return P.add(eng, lambda e: e.tensor_copy(out=out, in_=in_), reads, writes)


def op_mm(P, out, pairs, reads, writes):
    def fn(e):
        n = len(pairs)
        ins = None
        for i, (l, r) in enumerate(pairs):
            ins = e.matmul(out, lhsT=l, rhs=r, start=(i == 0), stop=(i == n - 1))
        return ins

    return P.add("pe", fn, reads, writes)


def op_tr(P, items, ident, reads, writes):
    def fn(e):
        ins = None
        for (o, i) in items:
            ins = e.transpose(o, i, ident)
        return ins

    return P.add("pe", fn, reads, writes)


def build_program(T):
    NTO = (T + 2 * HALO) // 512
    NTT = T // 512 + 1
    NOWN = T // 512
    TO = NTO * 512
    TTK = NTT * 512

    nc = bass.Bass("TRN2", target_bir_lowering=False)

    def din(name, shape, dt=F32):
        return nc.dram_tensor(name, list(shape), dt, kind="ExternalInput").ap()

    def dscr(name, shape, dt):
        kind = "ExternalOutput" if DEBUG_SCRATCH else "Internal"
        return nc.dram_tensor(name, list(shape), dt, kind=kind).ap()

    xo = din("xo", [TO, D])
    xt = din("xt", [TTK, D])
    y = nc.dram_tensor("y", [T, D], F32, kind="ExternalOutput").ap()
    w_in = din("w_in", [D, 12800])
    w_rnn = din("w_rnn", [D, D])
    w_att = din("w_att", [512, D])
    w_out = din("w_out", [D, D])
    w_up = din("w_up", [D, 8192])
    w_dn = din("w_dn", [8192, D])
    wg_src = din("wg", [2, 3, 16, 128, 128])
    prm = din("prm", [128, 23 * 16])
    gvec = din("gvec", [128, 2 * 16])
    gfin = din("gfin", [1, D])
    cst = din("cst", [TO, 128])
    msk = din("msk", [3, 128, 256])
    flg = din("flg", [128, 2])

    XN = dscr("XN", [16, 128, TO], BF16)
    XNo = dscr("XNo", [16, 128, TTK], BF16)
    XR = dscr("XR", [16, 128, T + 1024], F32)
    XRo = dscr("XRo", [16, 128, TTK], F32)
    GG = dscr("GG", [16, 128, T], F32)
    HG = dscr("HG", [16, 128, T], BF16)
    QT = dscr("QT", [12, 128, T], BF16)
    KT = dscr("KT", [12, 128, TO], BF16)
    VV = dscr("VV", [TO, 1536], BF16)
    AT = dscr("AT", [4, 128, T], BF16)
    WIG = dscr("WIG", [32, 128, 16, 128], BF16)
    WRN = dscr("WRN", [16, 128, 16, 128], BF16)
    WAT = dscr("WAT", [16, 128, 4, 128], BF16)
    WOU = dscr("WOU", [4, 128, 16, 512], BF16)
    WUP = dscr("WUP", [16, 128, 16, 512], BF16)
    WDN = dscr("WDN", [4, 128, 64, 512], BF16)

    AW = 52224
    with contextlib.ExitStack() as st:
        arena_t = st.enter_context(nc.sbuf_tensor("arena", [128, AW], F32))
        banks = [st.enter_context(nc.psum_tensor("pb%d" % i, [128, 512], F32)) for i in range(8)]
        A = Arena(arena_t, AW)
        P = Prog(nc)

        def bank(i):
            return banks[i][:]

        def bankb(i):
            return banks[i][:].bitcast(BF16)

        ident32 = A.f32(128)
        ident = A.bf16(128)
        ones = A.bf16(128)
        mask32 = A.f32(3 * 256)
        maskb = A.bf16(3 * 256)
        tiny = A.f32(4)
        flags = A.f32(2)
        soth = A.f32(16)
        initf = A.f32(16)
        initb = A.f32(16)
        gv = A.f32(32)
        A.set_base()

        P.add("pool", lambda e: e.memset(ident32, 1.0), writes=["ident32"])
        P.add("pool", lambda e: e.affine_select(out=ident32, in_=ident32, pattern=[[-1, 128]], compare_op=ALU.is_equal,
                                                fill=0.0, base=0, channel_multiplier=1), reads=["ident32"], writes=["ident32"])
        op_copy(P, "pool", ident, ident32, ["ident32"], ["ident"])
        P.add("pool", lambda e: e.memset(ones, 1.0), writes=["ones"])
        P.dma("sp", mask32.rearrange("p (m q) -> p m q", m=3), msk.rearrange("m p q -> p m q"), writes=["mask32"])
        op_copy(P, "pool", maskb, mask32, ["mask32"], ["maskb"])
        P.dma("sp", flags, flg, writes=["flags"])
        P.dma("sp", gv, gvec, writes=["gv"])
        P.add("pool", lambda e: e.memset(soth, 0.0), writes=["soth"])

        def phase_p1a():
            A.reset()
            gfull = A.f32(16 * 128).rearrange("p (k c) -> p k c", k=16)
            xs32 = [A.f32(D) for _ in range(8)]
            junk = A.bf16(D)
            xsb = [A.bf16(D) for _ in range(2)]
            xnTs = [A.bf16(16 * 512).rearrange("p (k t) -> p k t", k=16) for _ in range(2)]
            ssq = [A.f32(4) for _ in range(2)]
            ms = [A.f32(4) for _ in range(2)]
            rstd = [A.f32(4) for _ in range(2)]
            for k in range(16):
                op_copy(P, "pool", gfull[:, k, :], gv[:, k:k + 1].to_broadcast([128, 128]), ["gv"], [("gfull", k)])
            gkeys = [("gfull", k) for k in range(16)]
            cnt = 0
            for (src, ntile, dst, dname) in ((xo, NTO, XN, "XN"), (xt, NTT, XNo, "XNo")):
                for i in range(ntile):
                    par = cnt % 2
                    cnt += 1
                    for s in range(4):
                        b = xs32[par * 4 + s]
                        P.dma("sp", b, src[(i * 4 + s) * 128:(i * 4 + s + 1) * 128, :], writes=[("xs32", par, s)])
                        op_act(P, junk, b, AF.Square, [("xs32", par, s)], ["junk", ("ssq", par, s)],
                               accum=ssq[par][:, s:s + 1])
                    op_ts(P, "dve", ms[par], ssq[par], 1.0 / D, EPS, ALU.mult, ALU.add,
                          [("ssq", par, s) for s in range(4)], [("ms", par)])
                    op_act(P, ms[par], ms[par], AF.Sqrt, [("ms", par)], [("ms", par)])
                    P.add("dve", (lambda o, i_: (lambda e: e.reciprocal(out=o, in_=i_)))(rstd[par], ms[par]),
                          [("ms", par)], [("rstd", par)])
                    xn = xnTs[par]
                    for s in range(4):
                        sb_ = xsb[s % 2]
                        op_ts(P, "dve", sb_, xs32[par * 4 + s], rstd[par][:, s:s + 1], None, ALU.mult, None,
                              [("xs32", par, s), ("rstd", par)], [("xsb", s % 2)])
                        for h in range(2):
                            bk = 4 + h
                            op_tr(P, [(bankb(bk)[:, j * 128:(j + 1) * 128], sb_[:, (h * 8 + j) * 128:(h * 8 + j + 1) * 128])
                                      for j in range(8)], ident, [("xsb", s % 2), "ident"], [("bank", bk)])
                            op_tt(P, "dve", xn[:, h * 8:(h + 1) * 8, s * 128:(s + 1) * 128],
                                  bankb(bk).rearrange("p (k c) -> p k c", k=8), gfull[:, h * 8:(h + 1) * 8, :], ALU.mult,
                                  [("bank", bk)] + gkeys, [("xnT", par, s, h)])
                    P.dma("pool", dst[:, :, i * 512:(i + 1) * 512].rearrange("k p t -> p k t"), xn,
                          reads=[("xnT", par, s, h) for s in range(4) for h in range(2)], writes=[(dname, i)])

        def load_resident(Wres, col0, ncols, stage, tag):
            i = 0
            engs = ("act", "dve", "pool")
            for cb in range(ncols // 512):
                for kh in range(2):
                    sg = stage[i % 2].rearrange("p (k c) -> p k c", k=8)
                    P.dma("sp", sg, w_in[kh * 1024:(kh + 1) * 1024, col0 + cb * 512: col0 + (cb + 1) * 512]
                          .rearrange("(k p) c -> p k c", p=128), writes=[("wstage", i % 2)])
                    op_copy(P, engs[i % 3], Wres[:, kh * 8:(kh + 1) * 8, cb * 512:(cb + 1) * 512], sg,
                            [("wstage", i % 2)], [("wres", tag, cb, kh)])
                    i += 1
            return [("wres", tag, cb, kh) for cb in range(ncols // 512) for kh in range(2)]

        def phase_g_formA(tag, col0, tiles_list, evac):
            A.reset()
            Wres = A.bf16(16 * 2048).rearrange("p (k c) -> p k c", k=16)
            stage = [A.f32(8 * 512) for _ in range(2)]
            xb = [A.bf16(16 * 512).rearrange("p (k t) -> p k t", k=16) for _ in range(2)]
            wkeys = load_resident(Wres, col0, 2048, stage, tag)
            state = evac(None, None, None, None, init=True)
            wcg = make_wc() if tag == "xr" else None
            bk = 0

            def load_tile(ti):
                srcT, sname, i = tiles_list[ti]
                P.dma("sp", xb[ti % 2], srcT[:, :, i * 512:(i + 1) * 512].rearrange("k p t -> p k t"),
                      reads=[(sname, i)], writes=[("xb", ti % 2)])

            load_tile(0)
            for ti, (srcT, sname, i) in enumerate(tiles_list):
                x_ = xb[ti % 2]
                for oc in range(16):
                    b = bk % 4
                    bk += 1
                    op_mm(P, bank(b), [(Wres[:, k, oc * 128:(oc + 1) * 128], x_[:, k, :]) for k in range(16)],
                          [("xb", ti % 2)] + wkeys, [("bank", b)])
                    evac(b, oc, ti, (sname, i), state=state)
                    if oc == 0 and ti + 1 < len(tiles_list):
                        load_tile(ti + 1)
                    if wcg is not None and oc in (2, 7, 12):
                        next(wcg, None)
            if wcg is not None:
                for _ in wcg:
                    pass

        def make_evac_xr():
            def evac(b, oc, ti, info, init=False, state=None):
                if init:
                    return dict(stg=[A.f32(4 * 512).rearrange("p (k t) -> p k t", k=4) for _ in range(2)], n=0)
                sname, i = info
                g = state["n"] // 4
                stg = state["stg"][g % 2]
                j = oc % 4
                op_copy(P, "act" if oc % 2 == 0 else "dve", stg[:, j, :], bank(b), [("bank", b)], [("stg", g % 2, j)])
                state["n"] += 1
                if j == 3:
                    if sname == "XN":
                        dst = XR[oc - 3:oc + 1, :, (i - 1) * 512:i * 512]
                        wk = ("XR", oc // 4, i)
                    else:
                        dst = XRo[oc - 3:oc + 1, :, i * 512:(i + 1) * 512]
                        wk = ("XRo", oc // 4, i)
                    P.dma("pool", dst.rearrange("k p t -> p k t"), stg, reads=[("stg", g % 2, jj) for jj in range(4)],
                          writes=[wk])
            return evac

        def make_evac_gelu():
            def evac(b, oc, ti, info, init=False, state=None):
                if init:
                    return dict(stg=[A.f32(4 * 512).rearrange("p (k t) -> p k t", k=4) for _ in range(2)], n=0,
                                t1=[A.f32(512) for _ in range(3)], t2=[A.f32(512) for _ in range(3)])
                sname, i = info
                n = state["n"]
                g = n // 4
                stg = state["stg"][g % 2]
                j = oc % 4
                r3 = n % 3
                t1 = state["t1"][r3]
                t2 = state["t2"][r3]
                op_act(P, t1, bank(b), AF.Square, [("bank", b)], [("t1", r3)])
                op_act(P, t2, bank(b), AF.Copy, [("bank", b)], [("t2", r3)], scale=0.5)
                op_ts(P, "dve", t1, t1, 0.044715, 1.0, ALU.mult, ALU.add, [("t1", r3)], [("t1", r3)])
                op_tt(P, "pool", t1, t1, t2, ALU.mult, [("t1", r3), ("t2", r3)], [("t1", r3)])
                op_act(P, t1, t1, AF.Tanh, [("t1", r3)], [("t1", r3)], scale=1.5957691216057308)
                op_stt(P, stg[:, j, :], t1, 1.0, t2, ALU.add, ALU.mult, [("t1", r3), ("t2", r3)], [("stg", g % 2, j)])
                state["n"] += 1
                if j == 3:
                    P.dma("pool", GG[oc - 3:oc + 1, :, (i - 2) * 512:(i - 1) * 512].rearrange("k p t -> p k t"), stg,
                          reads=[("stg", g % 2, jj) for jj in range(4)], writes=[("GG", oc // 4, i - 2)])
            return evac

        def phase_g_formB(tag, col0, tiles, kind):
            A.reset()
            Wres = A.bf16(16 * 1536).rearrange("p (k c) -> p k c", k=16)
            stage = [A.f32(8 * 512) for _ in range(2)]
            xb = [A.bf16(16 * 512).rearrange("p (k t) -> p k t", k=16) for _ in range(2)]
            wkeys = load_resident(Wres, col0, 1536, stage, tag)
            if kind in ("q", "k"):
                cs = [A.f32(4 * 128).rearrange("p (s x) -> p s x", s=4) for _ in range(2)]
                qb = [[A.bf16(512) for _ in range(12)] for _ in range(2)]
                rt = [[A.f32(64).rearrange("p (h e) -> p h e", h=4) for _ in range(4)] for _ in range(2)]
                outs = [A.bf16(12 * 512).rearrange("p (h t) -> p h t", h=12) for _ in range(2)]
            else:
                outs = [A.bf16(4 * 1536).rearrange("p (s c) -> p s c", s=4) for _ in range(2)]
            bk = 0
            n = 0
            for ti, i in enumerate(tiles):
                x_ = xb[ti % 2]
                P.dma("sp", x_, XN[:, :, i * 512:(i + 1) * 512].rearrange("k p t -> p k t"),
                      reads=[("XN", i)], writes=[("xb", ti % 2)])
                ot = outs[ti % 2]
                okeys = []
                if kind in ("q", "k"):
                    c_ = cs[ti % 2]
                    P.dma("sp", c_, cst[i * 512:(i + 1) * 512, :].rearrange("(s p) x -> p s x", p=128), writes=[("cs", ti % 2)])
                    if kind == "q":
                        cf = c_.rearrange("p s x -> p (s x)")
                        op_ts(P, "pool", cf, cf, QSCALE, None, ALU.mult, None, [("cs", ti % 2)], [("cs", ti % 2)])
                for cb in range(3):
                    for s in range(4):
                        b = bk % 4
                        bk += 1
                        op_mm(P, bank(b), [(x_[:, k, s * 128:(s + 1) * 128], Wres[:, k, cb * 512:(cb + 1) * 512]) for k in range(16)],
                              [("xb", ti % 2)] + wkeys, [("bank", b)])
                        if kind == "v":
                            kk = ("outs", ti % 2, cb, s)
                            op_copy(P, "act" if n % 2 == 0 else "dve", ot[:, s, cb * 512:(cb + 1) * 512], bank(b), [("bank", b)], [kk])
                            okeys.append(kk)
                            n += 1
                            continue
                        r2 = n % 2
                        n += 1
                        qi = cb * 4 + s
                        qb_ = qb[ti % 2][qi]
                        qk = ("qb", ti % 2, qi)
                        op_act(P, qb_, bank(b), AF.Copy, [("bank", b)], [qk], scale=(QSCALE if kind == "q" else 1.0))
                        p4 = bank(b).rearrange("p (h e) -> p h e", h=4)
                        qb4 = qb_.rearrange("p (h e) -> p h e", h=4)
                        cos4 = c_[:, s, 0:64].rearrange("p (h e) -> p h e", h=4)
                        sin4 = c_[:, s, 64:128].rearrange("p (h e) -> p h e", h=4)
                        t1 = p4[:, :, 0:16]
                        t2 = p4[:, :, 16:32]
                        ra, rb, rc, rd = rt[r2]
                        rk = [("rt", r2, z) for z in range(4)]
                        op_tt(P, "dve", ra, t1, cos4, ALU.mult, [("bank", b), ("cs", ti % 2)], [rk[0]])
                        op_tt(P, "dve", rb, t2, sin4, ALU.mult, [("bank", b), ("cs", ti % 2)], [rk[1]])
                        op_tt(P, "dve", rc, t2, cos4, ALU.mult, [("bank", b), ("cs", ti % 2)], [rk[2]])
                        op_tt(P, "dve", rd, t1, sin4, ALU.mult, [("bank", b), ("cs", ti % 2)], [rk[3]])
                        op_tt(P, "dve", qb4[:, :, 0:16], ra, rb, ALU.subtract, [rk[0], rk[1], qk], [qk])
                        op_tt(P, "dve", qb4[:, :, 16:32], rc, rd, ALU.add, [rk[2], rk[3], qk], [qk])
                if kind in ("q", "k"):
                    for g6 in range(6):
                        tb = 4 + (g6 % 2)
                        items = []
                        rkeys = ["ident"]
                        for u in range(2):
                            qi = g6 * 2 + u
                            rkeys.append(("qb", ti % 2, qi))
                            for h in range(4):
                                items.append((bankb(tb)[:, (u * 4 + h) * 128:(u * 4 + h + 1) * 128],
                                              qb[ti % 2][qi][:, h * 128:(h + 1) * 128]))
                        op_tr(P, items, ident, rkeys, [("bank", tb)])
                        for u in range(2):
                            qi = g6 * 2 + u
                            cb, s = qi // 4, qi % 4
                            kk = ("outs", ti % 2, cb, s)
                            op_copy(P, "act" if u == 0 else "dve", ot[:, cb * 4:(cb + 1) * 4, s * 128:(s + 1) * 128],
                                    bankb(tb)[:, u * 512:(u + 1) * 512].rearrange("p (h t) -> p h t", h=4), [("bank", tb)], [kk])
                            okeys.append(kk)
                if kind == "q":
                    P.dma("pool", QT[:, :, (i - 2) * 512:(i - 1) * 512].rearrange("h p t -> p h t"), ot, reads=okeys, writes=[("QT", i - 2)])
                elif kind == "k":
                    P.dma("pool", KT[:, :, i * 512:(i + 1) * 512].rearrange("h p t -> p h t"), ot, reads=okeys, writes=[("KT", i)])
                else:
                    P.dma("pool", VV[i * 512:(i + 1) * 512, :].rearrange("(s p) c -> p s c", p=128), ot, reads=okeys, writes=[("VV", i)])

        def make_wc():
            NS = 2
            stage = [A.f32(8 * 512) for _ in range(NS)]
            bst = [A.bf16(8 * 512) for _ in range(NS)]
            engs = ("pool", "act", "dve")
            cnt = [0]

            def conv(src, K, c0, ncols, dst, PW, dname):
                kcs = K // 128
                for cb in range(ncols // 512):
                    for kh in range((kcs + 7) // 8):
                        nk = min(8, kcs - kh * 8)
                        r = cnt[0] % NS
                        cnt[0] += 1
                        sg = stage[r][:, 0:nk * 512].rearrange("p (k c) -> p k c", k=nk)
                        P.dma("sp", sg, src[kh * 1024:kh * 1024 + nk * 128, c0 + cb * 512:c0 + (cb + 1) * 512]
                              .rearrange("(k p) c -> p k c", p=128), writes=[("wcs", r)])
                        npn = 512 // PW
                        bo = bst[r][:, 0:nk * 512]
                        op_copy(P, engs[cnt[0] % 3], bo.rearrange("p (n k c) -> p k n c", n=npn, k=nk),
                                sg.rearrange("p k (n c) -> p k n c", n=npn), [("wcs", r)], [("wcb", r)])
                        P.dma("pool", dst[cb * npn:(cb + 1) * npn, :, kh * 8:kh * 8 + nk, :].rearrange("n p k c -> p n k c"),
                              bo.rearrange("p (n k c) -> p n k c", n=npn, k=nk), reads=[("wcb", r)], writes=[(dname, cb, kh)])
                        yield

            def gen():
                yield from conv(w_in, 2048, 8704, 4096, WIG, 128, "WIG")
                yield from conv(w_rnn, 2048, 0, 2048, WRN, 128, "WRN")
                yield from conv(w_att, 512, 0, 2048, WAT, 128, "WAT")
                yield from conv(w_out, 2048, 0, 2048, WOU, 512, "WOU")
                yield from conv(w_up, 2048, 0, 8192, WUP, 512, "WUP")
                yield from conv(w_dn, 8192, 0, 2048, WDN, 512, "WDN")
            return gen()

        def phase_p2():
            A.reset()
            NB = T // 1024
            pr = A.f32(23 * 16)
            prv = pr.rearrange("p (n c) -> p n c", n=23)
            hba = A.f32(48).rearrange("p (n c) -> p n c", n=3)
            hbi = A.f32(48).rearrange("p (n c) -> p n c", n=3)
            ncs = A.f32(48).rearrange("p (n c) -> p n c", n=3)
            hncs = A.f32(48).rearrange("p (n c) -> p n c", n=3)
            NSET = 4
            wgs = [A.f32(128 * 6).rearrange("p (g d o) -> p g d o", g=2, d=3) for _ in range(2)]
            wgbs = [A.bf16(128 * 6).rearrange("p (g d o) -> p g d o", g=2, d=3) for _ in range(2)]
            xr = A.f32(T + 8)
            xc = A.f32(T)
            xcb = A.bf16(T)
            hf = A.f32(T)
            dgs = [[A.f32(128) for _ in range(5)] for _ in range(2)]
            Ab = [A.f32(1024) for _ in range(NSET)]
            Bb = [A.f32(1024) for _ in range(NSET)]
            Cb = [A.f32(1024) for _ in range(NSET)]
            ggb = [A.f32(1024) for _ in range(2)]
            HBb = [A.f32(1024) for _ in range(2)]
            OUb = [A.bf16(1024) for _ in range(2)]

            P.dma("sp", pr, prm, writes=["pr"])
            op_ts(P, "dve", hba.rearrange("p n c -> p (n c)"), pr[:, 160:208], 0.5, None, ALU.mult, None, ["pr"], ["hba"])
            op_ts(P, "dve", hbi.rearrange("p n c -> p (n c)"), pr[:, 208:256], 0.5, None, ALU.mult, None, ["pr"], ["hbi"])
            ncf = ncs.rearrange("p n c -> p (n c)")
            op_act(P, ncf, pr[:, 256:304], AF.Exp, ["pr"], ["ncs"], scale=-1.0)
            op_act(P, ncf, ncf, AF.Ln, ["ncs"], ["ncs"], bias=1.0)
            op_ts(P, "dve", ncf, ncf, -8.0, None, ALU.mult, None, ["ncs"], ["ncs"])
            op_ts(P, "dve", hncs.rearrange("p n c -> p (n c)"), ncf, 0.5, None, ALU.mult, None, ["ncs"], ["hncs"])
            def load_gates(c):
                sl = c % 2
                P.dma("sp", wgs[sl], wg_src[:, :, c].rearrange("g d i o -> i g d o"), writes=[("wgs", sl)])
                op_copy(P, "pool", wgbs[sl], wgs[sl], [("wgs", sl)], [("wgb", sl)])
                return wgbs[sl], ("wgb", sl)

            pkeys = ["hba", "hbi", "ncs", "hncs", "pr"]
            bkc = [0]
            blk = [0]
            outc = [0]

            def run_dir(c, d, reverse, ntok, init_ap, init_keys, emit_out, wgb, wgk, conv_fn=None):
                nb = ntok // 1024
                order = list(range(nb - 1, -1, -1) if reverse else range(nb))
                prev = None
                if conv_fn is not None:
                    conv_fn(c, order[0])
                for p0 in range(0, nb, 2):
                    info = []
                    for bi in order[p0:p0 + 2]:
                        r = blk[0] % NSET
                        blk[0] += 1
                        Ax, Bx, Cx = Ab[r], Bb[r], Cb[r]
                        if conv_fn is not None:
                            nxt = order.index(bi) + 1
                            if nxt < nb:
                                conv_fn(c, order[nxt])
                        for tl in range(2):
                            t0 = bi * 1024 + tl * 512
                            ba = 4 + bkc[0] % 4
                            bb = 4 + (bkc[0] + 1) % 4
                            bkc[0] += 2
                            op_mm(P, bank(ba), [(wgb[:, 0, d, :], xcb[:, t0:t0 + 512])], [wgk, ("xcb", bi, tl)], [("bank", ba)])
                            op_mm(P, bank(bb), [(wgb[:, 1, d, :], xcb[:, t0:t0 + 512])], [wgk, ("xcb", bi, tl)], [("bank", bb)])
                            op_act(P, Ax[:, tl * 512:(tl + 1) * 512], bank(ba), AF.Tanh, [("bank", ba)] + pkeys, [("A", r, tl)],
                                   scale=0.5, bias=hba[:, d, c:c + 1])
                            op_act(P, Bx[:, tl * 512:(tl + 1) * 512], bank(bb), AF.Tanh, [("bank", bb)] + pkeys, [("B", r, tl)],
                                   scale=0.5, bias=hbi[:, d, c:c + 1])
                        ak = [("A", r, 0), ("A", r, 1)]
                        bkeys = [("B", r, 0), ("B", r, 1)]
                        if reverse:
                            op_act(P, Ax, Ax, AF.Exp, ak + pkeys, ak, scale=hncs[:, d, c:c + 1], bias=hncs[:, d, c:c + 1])
                            op_tt(P, "dve", Cx, Ax, Ax, ALU.mult, ak, [("C", r)])
                        else:
                            op_act(P, Cx, Ax, AF.Exp, ak + pkeys, [("C", r)], scale=ncs[:, d, c:c + 1], bias=ncs[:, d, c:c + 1])
                            op_act(P, Ax, Ax, AF.Exp, ak + pkeys, ak, scale=hncs[:, d, c:c + 1], bias=hncs[:, d, c:c + 1])
                        info.append((bi, r, Ax, Bx, Cx, ak, bkeys))
                    for (bi, r, Ax, Bx, Cx, ak, bkeys) in info:
                        op_act(P, Cx, Cx, AF.Sqrt, [("C", r)], [("C", r)], scale=-0.25, bias=0.25)
                    for (bi, r, Ax, Bx, Cx, ak, bkeys) in info:
                        op_stt(P, Bx, Bx, 1.0, xc[:, bi * 1024:(bi + 1) * 1024], ALU.add, ALU.mult,
                               bkeys + [("xc", bi, 0), ("xc", bi, 1)], bkeys)
                        op_tt(P, "dve", Bx, Bx, Cx, ALU.mult, bkeys + [("C", r)], bkeys)
                        if prev is None:
                            ini, inik = init_ap, init_keys
                        else:
                            ini, inik = prev
                        if not reverse:
                            o = hf[:, bi * 1024:(bi + 1) * 1024]
                            P.add("dve", (lambda o_, a_, b_, i_: (lambda e: e.tensor_tensor_scan(
                                out=o_, data0=a_, data1=b_, initial=i_, op0=ALU.mult, op1=ALU.add)))(o, Ax, Bx, ini),
                                ak + bkeys + inik, [("hf", bi)])
                            prev = (hf[:, (bi + 1) * 1024 - 1:(bi + 1) * 1024], [("hf", bi)])
                        else:
                            ro = outc[0] % 2
                            outc[0] += 1
                            hbuf = HBb[ro]
                            P.add("dve", (lambda o_, a_, b_, i_: (lambda e: e.tensor_tensor_scan(
                                out=o_, data0=a_, data1=b_, initial=i_, op0=ALU.mult, op1=ALU.add)))(
                                hbuf[:, ::-1], Ax[:, ::-1], Bx[:, ::-1], ini),
                                ak + bkeys + inik, [("hb", ro)])
                            prev = (hbuf[:, 0:1], [("hb", ro)])
                            emit_out(bi, r, ro, hbuf, Cx)

            def load_own(c):
                P.dma("sp", xr[:, 0:T + 4], XR[c, :, 510:510 + T + 4],
                      reads=[("XR", c // 4, i) for i in range(1, NTO - 1)], writes=["xr"])

            def load_oth(c):
                P.add("pool", lambda e: e.memset(xr[:, 0:2], 0.0), writes=["xr"])
                P.dma("sp", xr[:, 2:T + 6], XRo[c, :, 0:T + 4], reads=[("XRo", c // 4, i) for i in range(NTT)], writes=["xr"])

            cvb = [0]

            def make_diags(c, ntap, w0):
                par = c % 2
                for j in range(ntap):
                    op_ts(P, "dve", dgs[par][j], ident32, prv[:, w0 + j, c:c + 1], None, ALU.mult, None,
                          ["ident32", "pr"], [("dg", par, j)])

            def conv_blk(c, bi, ntap, w0):
                par = c % 2
                for tl in range(2):
                    o = bi * 1024 + tl * 512
                    cb_ = cvb[0] % 4
                    cvb[0] += 1
                    op_mm(P, bank(cb_), [(dgs[par][j], xr[:, o + j:o + j + 512]) for j in range(ntap)],
                          ["xr"] + [("dg", par, j) for j in range(ntap)], [("bank", cb_)])
                    op_ts(P, "dve", xc[:, o:o + 512], bank(cb_), prv[:, 4, c:c + 1], None, ALU.add, None,
                          [("bank", cb_), "pr"], [("xc", bi, tl)])
                    op_act(P, xcb[:, o:o + 512], xc[:, o:o + 512], AF.Copy, [("xc", bi, tl)], [("xcb", bi, tl)])

            def conv_own(c, bi):
                conv_blk(c, bi, 4, 0)

            def conv_oth(c, bi):
                conv_blk(c, bi, 5, 5)

            for c in range(16):
                wgb_, wgk_ = load_gates(c)
                make_diags(c, 5, 5)
                load_oth(c)
                run_dir(c, 2, False, T, 0.0, [], None, wgb_, wgk_, conv_fn=conv_oth)
                op_copy(P, "dve", soth[:, c:c + 1], hf[:, T - 1:T], [("hf", NB - 1)], ["soth"])
            op_ts(P, "dve", initf, soth, flags[:, 0:1], None, ALU.mult, None, ["soth", "flags"], ["initf"])
            op_ts(P, "dve", initb, soth, flags[:, 1:2], None, ALU.mult, None, ["soth", "flags"], ["initb"])

            for c in range(16):
                wgb_, wgk_ = load_gates(c)
                make_diags(c, 4, 0)
                load_own(c)
                run_dir(c, 0, False, T, initf[:, c:c + 1], ["initf"], None, wgb_, wgk_, conv_fn=conv_own)

                def emit_out(bi, r, ro, hbuf, Cx, c=c):
                    g_ = ggb[ro]
                    P.dma("sp", g_, GG[c, :, bi * 1024:(bi + 1) * 1024],
                          reads=[("GG", c // 4, i) for i in range(bi * 2, bi * 2 + 2)], writes=[("gg", ro)])
                    op_tt(P, "dve", Cx, hbuf, hf[:, bi * 1024:(bi + 1) * 1024], ALU.add, [("hb", ro), ("hf", bi), ("C", r)], [("C", r)])
                    op_tt(P, "pool", OUb[ro], Cx, g_, ALU.mult, [("C", r), ("gg", ro)], [("ou", ro)])
                    P.dma("pool", HG[c, :, bi * 1024:(bi + 1) * 1024], OUb[ro], reads=[("ou", ro)], writes=[("HG", c, bi)])

                run_dir(c, 1, True, T, initb[:, c:c + 1], ["initb"], emit_out, wgb_, wgk_)

        def phase_p3():
            A.reset()
            NST = T // 2048
            Uacc = A.f32(4 * 2048).rearrange("p (j t) -> p j t", j=4)
            Dacc = A.f32(4 * 2048).rearrange("p (j t) -> p j t", j=4)
            qbuf = A.bf16(4 * 2048).rearrange("p (j t) -> p j t", j=4)
            kbuf = A.bf16(4 * 4096).rearrange("p (j t) -> p j t", j=4)
            vb = [A.bf16(512) for _ in range(3)]
            pt = [A.bf16(4 * 256).rearrange("p (j q) -> p j q", j=4) for _ in range(3)]
            atb = A.bf16(4 * 2048).rearrange("p (j t) -> p j t", j=4)
            mk = maskb.rearrange("p (m q) -> p m q", m=3)
            vcnt = [0]
            scnt = [0]
            for stile in range(NST):
                for g, d in enumerate((1, 4, 16)):
                    W = 2048 + 128 * d
                    B0 = 1024 + stile * 2048 - 64 * d
                    P.dma("sp", qbuf, QT[4 * g:4 * g + 4, :, stile * 2048:(stile + 1) * 2048].rearrange("h p t -> p h t"),
                          reads=[("QT", i) for i in range(stile * 4, stile * 4 + 4)], writes=["qbuf"])
                    P.dma("sp", kbuf[:, :, 0:W], KT[4 * g:4 * g + 4, :, B0:B0 + W].rearrange("h p t -> p h t"),
                          reads=[("KT", i) for i in range(NTO)], writes=["kbuf"])
                    nq = 16 // d
                    for r in range(d):
                        slots = {}

                        def do_chunk(m, r=r, d=d, g=g, stile=stile, nq=nq, B0=B0):
                            sl = vcnt[0] % 3
                            vcnt[0] += 1
                            row0 = B0 + r + 128 * d * m
                            P.dma("sp", vb[sl], VV[ss(row0, 128, d), 512 * g:512 * (g + 1)],
                                  reads=[("VV", i) for i in range(NTO)], writes=[("vb", sl)])
                            lo = 128 if m == 0 else 0
                            hi = 128 if m == nq else 256
                            if stile == 0 and m == 0:
                                mi = 1
                            elif stile == NST - 1 and m == nq:
                                mi = 2
                            else:
                                mi = 0
                            ptile = pt[sl]
                            for j in range(4):
                                sb_ = 4 + (scnt[0] % 4)
                                scnt[0] += 1
                                kc0 = r + 128 * d * m
                                qc0 = r + 128 * d * (m - 1) + lo * d
                                nqq = hi - lo
                                op_mm(P, bank(sb_)[:, lo:hi],
                                      [(kbuf[:, j, ss(kc0, 128, d)], qbuf[:, j, ss(qc0, nqq, d)])],
                                      ["kbuf", "qbuf"], [("bank", sb_)])
                                op_act(P, ptile[:, j, lo:hi], bank(sb_)[:, lo:hi], AF.Exp, [("bank", sb_)], [("pt", sl, j)])
                                op_tt(P, "pool", ptile[:, j, lo:hi], ptile[:, j, lo:hi], mk[:, mi, lo:hi], ALU.mult,
                                      [("pt", sl, j), "maskb"], [("pt", sl, j)])
                            slots[m] = sl

                        do_chunk(0)
                        for i in range(nq):
                            do_chunk(i + 1)
                            s0, s1 = slots[i], slots[i + 1]
                            pr_ = [("pt", s0, j) for j in range(4)] + [("pt", s1, j) for j in range(4)]

                            pb_ = 0 if (i % 2 == 0) else 2
                            db_ = pb_ + 1

                            def pv(e, s0=s0, s1=s1, pb_=pb_):
                                ins = None
                                for j in range(4):
                                    e.matmul(bank(pb_)[:, j * 128:(j + 1) * 128], lhsT=vb[s0][:, j * 128:(j + 1) * 128],
                                             rhs=pt[s0][:, j, 128:256], start=True, stop=False)
                                    ins = e.matmul(bank(pb_)[:, j * 128:(j + 1) * 128], lhsT=vb[s1][:, j * 128:(j + 1) * 128],
                                                   rhs=pt[s1][:, j, 0:128], start=False, stop=True)
                                return ins

                            def dn(e, s0=s0, s1=s1, db_=db_):
                                ins = None
                                for j in range(4):
                                    e.matmul(bank(db_)[:, j * 128:(j + 1) * 128], lhsT=ones, rhs=pt[s0][:, j, 128:256],
                                             start=True, stop=False)
                                    ins = e.matmul(bank(db_)[:, j * 128:(j + 1) * 128], lhsT=ones, rhs=pt[s1][:, j, 0:128],
                                                   start=False, stop=True)
                                return ins

                            P.add("pe", pv, pr_ + [("vb", s0), ("vb", s1)], [("bank", pb_)])
                            P.add("pe", dn, pr_ + ["ones"], [("bank", db_)])
                            c0 = r + 128 * d * i
                            usl = Uacc[:, :, ss(c0, 128, d)]
                            dsl = Dacc[:, :, ss(c0, 128, d)]
                            b6 = bank(pb_).rearrange("p (j q) -> p j q", j=4)
                            b7 = bank(db_).rearrange("p (j q) -> p j q", j=4)
                            if g == 0:
                                op_copy(P, "dve", usl, b6, [("bank", pb_)], ["Uacc"])
                                op_copy(P, "act", dsl, b7, [("bank", db_)], ["Dacc"])
                            else:
                                op_tt(P, "dve", usl, usl, b6, ALU.add, [("bank", pb_), "Uacc"], ["Uacc"])
                                op_tt(P, "dve", dsl, dsl, b7, ALU.add, [("bank", db_), "Dacc"], ["Dacc"])
                Df = Dacc.rearrange("p j t -> p (j t)")
                Uf = Uacc.rearrange("p j t -> p (j t)")
                P.add("dve", lambda e: e.reciprocal(out=Df, in_=Df), ["Dacc"], ["Dacc"])
                op_tt(P, "dve", atb.rearrange("p j t -> p (j t)"), Uf, Df, ALU.mult, ["Uacc", "Dacc"], ["atb"])
                P.dma("pool", AT[:, :, stile * 2048:(stile + 1) * 2048].rearrange("j p t -> p j t"), atb,
                      reads=["atb"], writes=[("AT", stile)])

        def phase_p4():
            A.reset()
            gF = A.f32(D)
            R1 = A.off
            xnT = A.bf16(16 * 512).rearrange("p (k t) -> p k t", k=16)
            hgT = A.bf16(16 * 512).rearrange("p (k t) -> p k t", k=16)
            atT = A.bf16(4 * 512).rearrange("p (k t) -> p k t", k=4)
            mxb = A.bf16(16 * 512).rearrange("p (k t) -> p k t", k=16)
            A.off = R1
            HT = A.bf16(64 * 512).rearrange("p (k t) -> p k t", k=64)
            H1 = A.f32(4 * D).rearrange("p (s c) -> p s c", s=4)
            h1nT = A.bf16(16 * 512).rearrange("p (k t) -> p k t", k=16)
            hsb = [A.bf16(D) for _ in range(1)]
            junk = A.bf16(D)
            wp = [A.bf16(16 * 512) for _ in range(3)]
            tA = [A.f32(512) for _ in range(2)]
            tB = [A.f32(512) for _ in range(2)]
            rl = [A.f32(512) for _ in range(2)]
            ssq = A.f32(4)
            ms = A.f32(4)
            rstd = A.f32(4)
            ssq2 = A.f32(4)
            ms2 = A.f32(4)
            rstd2 = A.f32(4)
            P.dma("sp", gF, gfin.partition_broadcast(128), writes=["gF"])
            htkeys = [("HT", c) for c in range(64)]
            wpc = [0]
            wqc = [0]
            bkc = [0]

            def load_panel(src_ap, nelem, rkeys):
                sl = wpc[0] % 3
                wpc[0] += 1
                dst = wp[sl][:, 0:nelem]
                P.dma("sp", dst, src_ap, reads=rkeys, writes=[("wp", sl, j) for j in range(4)])
                return sl

            for t in range(NOWN):
                c0 = t * 512
                P.dma("sp", xnT, XN[:, :, (t + 2) * 512:(t + 3) * 512].rearrange("k p t -> p k t"),
                      reads=[("XN", t + 2)], writes=["xnT"] + htkeys)
                P.dma("sp", hgT, HG[:, :, c0:c0 + 512].rearrange("k p t -> p k t"),
                      reads=[("HG", c, t // 2) for c in range(16)], writes=["hgT"] + htkeys)
                P.dma("sp", atT, AT[:, :, c0:c0 + 512].rearrange("k p t -> p k t"),
                      reads=[("AT", t // 4)], writes=["atT"] + htkeys)
                for s in range(4):
                    P.dma("sp", H1[:, s, :], xo[HALO + c0 + s * 128:HALO + c0 + (s + 1) * 128, :], writes=[("H1", s)])
                for c in range(16):
                    sl = wpc[0] % 3
                    wpc[0] += 1
                    wq_ = wp[sl]
                    w_r = wq_[:, 0:2048].rearrange("p (k c) -> p k c", k=16)
                    w_gr = wq_[:, 2048:4096].rearrange("p (k c) -> p k c", k=16)
                    w_ga = wq_[:, 4096:6144].rearrange("p (k c) -> p k c", k=16)
                    w_a = wq_[:, 6144:6656].rearrange("p (k c) -> p k c", k=4)
                    P.dma("sp", w_r, WRN[c], writes=[("wp", sl, 0)])
                    P.dma("sp", w_gr, WIG[c], writes=[("wp", sl, 1)])
                    P.dma("sp", w_ga, WIG[16 + c], writes=[("wp", sl, 2)])
                    P.dma("sp", w_a, WAT[c], writes=[("wp", sl, 3)])
                    r2 = c % 2
                    b0 = 4 * (c % 2)
                    op_mm(P, bank(b0), [(w_r[:, k, :], hgT[:, k, :]) for k in range(16)], [("wp", sl, 0), "hgT"], [("bank", b0)])
                    op_mm(P, bank(b0 + 1), [(w_gr[:, k, :], xnT[:, k, :]) for k in range(16)], [("wp", sl, 1), "xnT"], [("bank", b0 + 1)])
                    op_act(P, tA[r2], bank(b0 + 1), AF.Tanh, [("bank", b0 + 1)], [("tA", r2)], scale=0.5)
                    op_stt(P, tA[r2], tA[r2], 1.0, bank(b0), ALU.add, ALU.mult, [("tA", r2), ("bank", b0)], [("tA", r2)])
                    op_mm(P, bank(b0 + 2), [(w_a[:, k, :], atT[:, k, :]) for k in range(4)], [("wp", sl, 3), "atT"], [("bank", b0 + 2)])
                    op_mm(P, bank(b0 + 3), [(w_ga[:, k, :], xnT[:, k, :]) for k in range(16)], [("wp", sl, 2), "xnT"], [("bank", b0 + 3)])
                    op_act(P, tB[r2], bank(b0 + 3), AF.Tanh, [("bank", b0 + 3)], [("tB", r2)], scale=0.5)
                    op_stt(P, tB[r2], tB[r2], 1.0, bank(b0 + 2), ALU.add, ALU.mult, [("tB", r2), ("bank", b0 + 2)], [("tB", r2)])
                    op_tt(P, "pool", tA[r2], tA[r2], tB[r2], ALU.add, [("tA", r2), ("tB", r2)], [("tA", r2)])
                    op_act(P, mxb[:, c, :], tA[r2], AF.Copy, [("tA", r2)], [("mxb", c)], scale=0.5)
                mxkeys = [("mxb", c) for c in range(16)]
                for cb in range(4):
                    sl = load_panel(WOU[cb].rearrange("p k c -> p (k c)"), 16 * 512, [])
                    wv = wp[sl].rearrange("p (k c) -> p k c", k=16)
                    for s in range(4):
                        b = bkc[0] % 4
                        bkc[0] += 1
                        op_mm(P, bank(b), [(mxb[:, k, s * 128:(s + 1) * 128], wv[:, k, :]) for k in range(16)],
                              mxkeys + [("wp", sl, j) for j in range(4)], [("bank", b)])
                        op_tt(P, "dve", H1[:, s, cb * 512:(cb + 1) * 512], H1[:, s, cb * 512:(cb + 1) * 512], bank(b), ALU.add,
                              [("bank", b), ("H1", s)], [("H1", s)])
                for s in range(4):
                    op_act(P, junk, H1[:, s, :], AF.Square, [("H1", s)], ["junk", ("ssq", s)], accum=ssq[:, s:s + 1])
                op_ts(P, "dve", ms, ssq, 1.0 / D, EPS, ALU.mult, ALU.add, [("ssq", s) for s in range(4)], ["ms"])
                op_act(P, ms, ms, AF.Sqrt, ["ms"], ["ms"])
                P.add("dve", lambda e: e.reciprocal(out=rstd, in_=ms), ["ms"], ["rstd"])
                for s in range(4):
                    sb_ = hsb[0]
                    op_act(P, sb_, H1[:, s, :], AF.Copy, [("H1", s), "rstd"], [("hsb", 0)], scale=rstd[:, s:s + 1])
                    for h in range(2):
                        bk = 4 + h
                        op_tr(P, [(bankb(bk)[:, j * 128:(j + 1) * 128], sb_[:, (h * 8 + j) * 128:(h * 8 + j + 1) * 128])
                                  for j in range(8)], ident, [("hsb", 0), "ident"], [("bank", bk)])
                        for j in range(8):
                            k = h * 8 + j
                            if h == 0:
                                op_ts(P, "dve", h1nT[:, k, s * 128:(s + 1) * 128], bankb(bk)[:, j * 128:(j + 1) * 128],
                                      gv[:, 16 + k:17 + k], None, ALU.mult, None, [("bank", bk), "gv"], [("h1nT", s, h, j)])
                            else:
                                op_act(P, h1nT[:, k, s * 128:(s + 1) * 128], bankb(bk)[:, j * 128:(j + 1) * 128], AF.Copy,
                                       [("bank", bk), "gv"], [("h1nT", s, h, j)], scale=gv[:, 16 + k:17 + k])
                hnkeys = [("h1nT", s, h, j) for s in range(4) for h in range(2) for j in range(8)]
                for pn in range(16):
                    sl = load_panel(WUP[pn].rearrange("p k c -> p (k c)"), 16 * 512, [])
                    wv = wp[sl].rearrange("p (k c) -> p k c", k=16)
                    for oc in range(4):
                        c = pn * 4 + oc
                        b = bkc[0] % 4
                        bkc[0] += 1
                        r3 = c % 2
                        op_mm(P, bank(b), [(wv[:, k, oc * 128:(oc + 1) * 128], h1nT[:, k, :]) for k in range(16)],
                              hnkeys + [("wp", sl, j) for j in range(4)], [("bank", b)])
                        op_act(P, rl[r3], bank(b), AF.Relu, [("bank", b)], [("rl", r3)])
                        op_tt(P, "pool", HT[:, c, :], rl[r3], rl[r3], ALU.mult, [("rl", r3)], [("HT", c)])
                for cb in range(4):
                    sls = []
                    for qd in range(4):
                        sls.append(None)
                    for qd in range(4):
                        sl = load_panel(WDN[cb, :, qd * 16:(qd + 1) * 16, :].rearrange("p k c -> p (k c)"), 16 * 512, [])
                        wv = wp[sl].rearrange("p (k c) -> p k c", k=16)
                        for s in range(4):
                            def grp(e, wv=wv, s=s, qd=qd):
                                ins = None
                                for k in range(16):
                                    ins = e.matmul(bank(s), lhsT=HT[:, qd * 16 + k, s * 128:(s + 1) * 128], rhs=wv[:, k, :],
                                                   start=(qd == 0 and k == 0), stop=(qd == 3 and k == 15))
                                return ins
                            P.add("pe", grp, [("HT", qd * 16 + k) for k in range(16)] + [("wp", sl, j) for j in range(4)], [("bank", s)])
                    for s in range(4):
                        op_tt(P, "dve", H1[:, s, cb * 512:(cb + 1) * 512], H1[:, s, cb * 512:(cb + 1) * 512], bank(s), ALU.add,
                              [("bank", s), ("H1", s)], [("H1", s)])
                    bkc[0] = 0
                for s in range(4):
                    op_act(P, junk, H1[:, s, :], AF.Square, [("H1", s)], ["junk", ("ssq2", s)], accum=ssq2[:, s:s + 1])
                op_ts(P, "dve", ms2, ssq2, 1.0 / D, EPS, ALU.mult, ALU.add, [("ssq2", s) for s in range(4)], ["ms2"])
                op_act(P, ms2, ms2, AF.Sqrt, ["ms2"], ["ms2"])
                P.add("dve", lambda e: e.reciprocal(out=rstd2, in_=ms2), ["ms2"], ["rstd2"])
                for s in range(4):
                    op_stt(P, H1[:, s, :], H1[:, s, :], rstd2[:, s:s + 1], gF, ALU.mult, ALU.mult, [("H1", s), "rstd2", "gF"], [("H1", s)])
                    P.dma("pool", y[c0 + s * 128:c0 + (s + 1) * 128, :], H1[:, s, :], reads=[("H1", s), "OUT"])

        phase_p1a()
        P.barrier(tiny)
        phase_g_formA("xr", 0, [(XN, "XN", i) for i in range(1, NTO - 1)] + [(XNo, "XNo", i) for i in range(NTT)], make_evac_xr())
        P.barrier(tiny)
        phase_g_formA("gr", 2048, [(XN, "XN", i) for i in range(2, NTO - 2)], make_evac_gelu())
        P.barrier(tiny)
        phase_g_formB("q", 4096, list(range(2, NTO - 2)), "q")
        P.barrier(tiny)
        phase_g_formB("k", 4096 + 1536, list(range(NTO)), "k")
        P.barrier(tiny)
        phase_g_formB("v", 4096 + 3072, list(range(NTO)), "v")
        P.barrier(tiny)
        phase_p2()
        P.barrier(tiny)
        phase_p3()
        P.barrier(tiny)
        phase_p4()
        P.add("sp", lambda e: None, writes=["OUT"])
        P.emit()
    return nc


def _pc(v):
    return np.ascontiguousarray(np.asarray(v, np.float32).reshape(16, 128).T)


def _rope_table(pos):
    inv_freq = (np.float32(500000.0) ** (-np.arange(0, 32, 2, dtype=np.float32) / np.float32(32))).astype(np.float32)
    ang = (pos.astype(np.float32)[:, None] * inv_freq[None, :]).astype(np.float32)
    cos = np.cos(ang).astype(np.float32)
    sin = np.sin(ang).astype(np.float32)
    return np.concatenate([np.tile(cos, (1, 4)), np.tile(sin, (1, 4))], axis=1).astype(np.float32)


def _masks(left_real, right_real):
    kk = np.arange(128)[:, None]
    qq = np.arange(256)[None, :]
    band = ((kk >= qq - 128) & (kk <= qq)).astype(np.float32)
    m_l = band if left_real else band * (kk >= 64)
    m_r = band if right_real else band * (kk < 64)
    return np.stack([band, m_l, m_r]).astype(np.float32)


_NC_CACHE = {}


def run_layer(T, seqs, P):
    if T not in _NC_CACHE:
        _NC_CACHE[T] = build_program(T)
    nc = _NC_CACHE[T]
    conv_w = np.asarray(P["conv_w"][0], np.float32)
    conv_b = np.asarray(P["conv_b"][0], np.float32)
    w_a = np.asarray(P["lru_w_a"][0], np.float32)
    w_i = np.asarray(P["lru_w_i"][0], np.float32)
    b_a = np.asarray(P["lru_b_a"][0], np.float32)
    b_i = np.asarray(P["lru_b_i"][0], np.float32)
    lam = np.asarray(P["lru_lambda"][0], np.float32)
    zeros = np.zeros(D, np.float32)
    shared = dict(
        w_in=np.ascontiguousarray(P["w_in"][0], np.float32), w_rnn=np.ascontiguousarray(P["w_rnn_proj"][0], np.float32),
        w_att=np.ascontiguousarray(P["w_attn_proj"][0], np.float32), w_out=np.ascontiguousarray(P["w_out"][0], np.float32),
        w_up=np.ascontiguousarray(P["w_up"][0], np.float32), w_dn=np.ascontiguousarray(P["w_down"][0], np.float32),
        gvec=np.ascontiguousarray(np.concatenate([_pc(P["mix_norm_g"][0]), _pc(P["mlp_norm_g"][0])], axis=1)),
        gfin=np.ascontiguousarray(np.asarray(P["final_norm_g"], np.float32).reshape(1, D)),
    )
    NTT = T // 512 + 1
    in_maps = []
    for (xs, p0) in seqs:
        S = xs.shape[0]
        xo = np.zeros((T + 2 * HALO, D), np.float32)
        lo = max(0, p0 - HALO)
        hi = min(S, p0 + T + HALO)
        xo[lo - (p0 - HALO):hi - (p0 - HALO)] = xs[lo:hi]
        xt = np.zeros((NTT * 512, D), np.float32)
        ff = fb = 0.0
        od = 0
        w5 = [conv_w[0], conv_w[1], conv_w[2], conv_w[3], zeros]
        if p0 > 0:
            assert p0 == T
            xt[0:T] = xs[0:T]
            xt[T:T + 2] = xs[T:T + 2]
            ff = 1.0
            od = 0
        elif S > T:
            xt[0:T] = xs[T:2 * T][::-1]
            xt[T] = xs[T - 1]
            xt[T + 1] = xs[T - 2]
            fb = 1.0
            od = 1
            w5 = [zeros, conv_w[3], conv_w[2], conv_w[1], conv_w[0]]
        wg = np.stack([np.stack([w_a[0], w_a[1], w_a[od]]), np.stack([w_i[0], w_i[1], w_i[od]])]).astype(np.float32)
        cols = [_pc(conv_w[j]) for j in range(4)] + [_pc(conv_b)] + [_pc(w) for w in w5]
        cols += [_pc(b_a[0]), _pc(b_a[1]), _pc(b_a[od])] + [_pc(b_i[0]), _pc(b_i[1]), _pc(b_i[od])]
        cols += [_pc(lam[0]), _pc(lam[1]), _pc(lam[od])] + [_pc(zeros)] * 4
        prm = np.ascontiguousarray(np.concatenate(cols, axis=1))
        pos = np.arange(p0 - HALO, p0 + T + HALO)
        m = dict(shared)
        m.update(xo=xo, xt=xt, wg=np.ascontiguousarray(wg), prm=prm, cst=_rope_table(np.maximum(pos, 0)),
                 msk=_masks(p0 > 0, p0 + T < S), flg=np.ascontiguousarray(np.tile(np.array([[ff, fb]], np.float32), (128, 1))))
        in_maps.append(m)
    res = run_bass_kernel_spmd(nc, in_maps, core_ids=list(range(8)))
    return res


def kernel(x_prompt, x_sample, mix_norm_g, w_in, conv_w, conv_b, lru_w_a, lru_b_a, lru_w_i, lru_b_i, lru_lambda,
           w_rnn_proj, w_attn_proj, w_out, mlp_norm_g, w_up, w_down, final_norm_g):
    x_prompt = np.asarray(x_prompt, np.float32)
    x_sample = np.asarray(x_sample, np.float32)
    T = x_prompt.shape[1]
    assert x_prompt.shape[0] == 4 and x_sample.shape[0] == 2 and x_sample.shape[1] == 2 * T
    P = dict(mix_norm_g=np.asarray(mix_norm_g), w_in=np.asarray(w_in), conv_w=np.asarray(conv_w), conv_b=np.asarray(conv_b),
             lru_w_a=np.asarray(lru_w_a), lru_b_a=np.asarray(lru_b_a), lru_w_i=np.asarray(lru_w_i), lru_b_i=np.asarray(lru_b_i),
             lru_lambda=np.asarray(lru_lambda), w_rnn_proj=np.asarray(w_rnn_proj), w_attn_proj=np.asarray(w_attn_proj),
             w_out=np.asarray(w_out), mlp_norm_g=np.asarray(mlp_norm_g), w_up=np.asarray(w_up), w_down=np.asarray(w_down),
             final_norm_g=np.asarray(final_norm_g))
    seqs = [(x_prompt[b], 0) for b in range(4)]
    for b in range(2):
        seqs += [(x_sample[b], 0), (x_sample[b], T)]
    res = run_layer(T, seqs, P)
    outs = [np.asarray(r["y"], np.float32) for r in res.results]
    y_prompt = np.stack(outs[0:4])
    y_sample = np.stack([np.concatenate([outs[4], outs[5]]), np.concatenate([outs[6], outs[7]])])
    return (y_prompt, y_sample)
```

```python
import contextlib
import math

import numpy as np
import concourse.bass as bass
import concourse.mybir as mybir
from concourse.bass_utils import run_bass_kernel_spmd

F32 = mybir.dt.float32
BF16 = mybir.dt.bfloat16
AF = mybir.ActivationFunctionType
ALU = mybir.AluOpType

D = 2048
KC = 16
HALO = 1024
EPS = 1e-6
QSCALE = 1.0 / math.sqrt(128.0)
DEBUG_SCRATCH = False


class Op:
    __slots__ = ("eng", "fn", "deps", "has_dep", "tick", "is_dma", "dsem", "dtarget", "ndma", "name")

    def __init__(self, eng, fn, is_dma, ndma, name):
        self.eng = eng
        self.fn = fn
        self.deps = set()
        self.has_dep = False
        self.tick = 0
        self.is_dma = is_dma
        self.dsem = None
        self.dtarget = 0
        self.ndma = ndma
        self.name = name


class Prog:
    ENGS = ("pe", "act", "dve", "pool", "sp")
    NDSEM = 24

    def __init__(self, nc):
        self.nc = nc
        self.ops = {e: [] for e in self.ENGS}
        self.key_w = {}
        self.key_r = {}
        self.dma_n = {e: 0 for e in self.ENGS}
        self.dma_tot = {}

    def add(self, eng, fn, reads=(), writes=(), dma=False, ndma=1, name=""):
        op = Op(eng, fn, dma, ndma, name)
        deps = op.deps
        bank_r = tuple(k for k in reads if isinstance(k, tuple) and k[0] == "bank")
        if bank_r:
            reads = tuple(k for k in reads if k not in bank_r)
            writes = tuple(writes) + tuple(k for k in bank_r if k not in writes)
        reads = tuple(reads) + ("PHASE",)
        for k in reads:
            w = self.key_w.get(k)
            if w is not None:
                deps.add(w)
        for k in writes:
            w = self.key_w.get(k)
            if w is not None:
                deps.add(w)
            for r in self.key_r.get(k, ()):
                deps.add(r)
        deps.discard(op)
        for d in deps:
            d.has_dep = True
        for k in reads:
            lst = self.key_r.setdefault(k, [])
            if not dma:
                for i, r in enumerate(lst):
                    if (not r.is_dma) and r.eng == eng:
                        lst[i] = op
                        break
                else:
                    lst.append(op)
            else:
                lst.append(op)
        for k in writes:
            self.key_w[k] = op
            self.key_r[k] = []
        if dma:
            slot = self.dma_n[eng] % self.NDSEM
            self.dma_n[eng] += 1
            op.dsem = (eng, slot)
            prev = self.dma_tot.get((eng, slot), 0)
            op.dtarget = prev + 16 * ndma
            self.dma_tot[(eng, slot)] = op.dtarget
        self.ops[eng].append(op)
        return op

    def dma(self, eng, out, in_, reads=(), writes=(), name=""):
        return self.add(eng, lambda e: [e.dma_start(out=out, in_=in_)], reads, writes, dma=True, ndma=1, name=name)

    def barrier(self, tiny):
        self.add("pool", lambda e: e.memset(tiny, 0.0), reads=(), writes=("PHASE",), name="barrier")

    def emit(self):
        nc = self.nc
        for e in self.ENGS:
            t = 0
            for op in self.ops[e]:
                if (not op.is_dma) and op.has_dep:
                    t += 1
                    op.tick = t
        with contextlib.ExitStack() as st:
            esem = {e: st.enter_context(nc.semaphore("s_" + e)) for e in ("pe", "act", "dve", "pool")}
            dsem = {}
            for e in ("sp", "act", "pool"):
                for s in range(min(self.NDSEM, self.dma_n[e])):
                    dsem[(e, s)] = st.enter_context(nc.semaphore("d_%s_%d" % (e, s)))
            block = st.enter_context(nc.Block())

            def run(ename):
                def body(eng):
                    waited = {}
                    for op in self.ops[ename]:
                        need = {}
                        for d in op.deps:
                            if d.is_dma:
                                k, v = d.dsem, d.dtarget
                            else:
                                k, v = d.eng, d.tick
                            if need.get(k, 0) < v:
                                need[k] = v
                        if op.is_dma:
                            g = op.dtarget - 16 * op.ndma
                            if g > 0 and need.get(op.dsem, 0) < g:
                                need[op.dsem] = g
                        for k, v in need.items():
                            if waited.get(k, 0) >= v:
                                continue
                            waited[k] = v
                            eng.wait_ge(dsem[k] if isinstance(k, tuple) else esem[k], v)
                        r = op.fn(eng)
                        if r is None:
                            continue
                        if op.is_dma:
                            assert len(r) == op.ndma, (op.name, len(r), op.ndma)
                            for ins in r:
                                ins.then_inc(dsem[op.dsem], 16)
                        elif op.has_dep:
                            ins = r[-1] if isinstance(r, (list, tuple)) else r
                            ins.then_inc(esem[ename], 1)

                return body

            block.tensor(run("pe"))
            block.scalar(run("act"))
            block.vector(run("dve"))
            block.gpsimd(run("pool"))
            block.sync(run("sp"))


class Arena:
    def __init__(self, t, words):
        self.t = t
        self.words = words
        self.off = 0
        self.base = 0

    def set_base(self):
        self.base = self.off

    def reset(self):
        self.off = self.base

    def f32(self, n):
        assert self.off + n <= self.words, ("arena overflow", self.off + n)
        ap = self.t[:, self.off:self.off + n]
        self.off += n
        return ap

    def bf16(self, n):
        w = (n + 1) // 2
        assert self.off + w <= self.words, ("arena overflow", self.off + w)
        ap = self.t[:, self.off:self.off + w].bitcast(BF16)
        self.off += w
        return ap


def ss(start, n, step):
    return slice(start, start + (n - 1) * step + 1, step)


def op_act(P, out, in_, func, reads, writes, scale=1.0, bias=0.0, accum=None):
    kw = dict(out=out, in_=in_, func=func, scale=scale, bias=bias)
    if accum is not None:
        kw["accum_out"] = accum
    return P.add("act", lambda e: e.activation(**kw), reads, writes)


def op_tt(P, eng, out, in0, in1, op, reads, writes):
    return P.add(eng, lambda e: e.tensor_tensor(out=out, in0=in0, in1=in1, op=op), reads, writes)


def op_ts(P, eng, out, in0, s1, s2, op0, op1, reads, writes):
    if s2 is None:
        return P.add(eng, lambda e: e.tensor_scalar(out=out, in0=in0, scalar1=s1, scalar2=None, op0=op0), reads, writes)
    return P.add(eng, lambda e: e.tensor_scalar(out=out, in0=in0, scalar1=s1, scalar2=s2, op0=op0, op1=op1), reads, writes)


def op_stt(P, out, in0, scalar, in1, op0, op1, reads, writes):
    return P.add("dve", lambda e: e.scalar_tensor_tensor(out=out, in0=in0, scalar=scalar, in1=in1, op0=op0, op1=op1), reads, writes)


def op_copy(P, eng, out, in_, reads, writes):
    if eng == "act":
        return op_act(P, out, in_, AF.Copy, reads, writes)
    return P.add(eng, lambda e: e.tensor_copy(out=out, in_=in_), reads, writes)


def op_mm(P, out, pairs, reads, writes):
    def fn(e):
        n = len(pairs)
        ins = None
        for i, (l, r) in enumerate(pairs):
            ins = e.matmul(out, lhsT=l, rhs=r, start=(i == 0), stop=(i == n - 1))
        return ins

    return P.add("pe", fn, reads, writes)


def op_tr(P, items, ident, reads, writes):
    def fn(e):
        ins = None
        for (o, i) in items:
            ins = e.transpose(o, i, ident)
        return ins

    return P.add("pe", fn, reads, writes)


def build_program(T):
    NTO = (T + 2 * HALO) // 512
    NTT = T // 512 + 1
    NOWN = T // 512
    TO = NTO * 512
    TTK = NTT * 512

    nc = bass.Bass("TRN2", target_bir_lowering=False)

    def din(name, shape, dt=F32):
        return nc.dram_tensor(name, list(shape), dt, kind="ExternalInput").ap()

    def dscr(name, shape, dt):
        kind = "ExternalOutput" if DEBUG_SCRATCH else "Internal"
        return nc.dram_tensor(name, list(shape), dt, kind=kind).ap()

    xo = din("xo", [TO, D])
    xt = din("xt", [TTK, D])
    y = nc.dram_tensor("y", [T, D], F32, kind="ExternalOutput").ap()
    w_in = din("w_in", [D, 12800])
    w_rnn = din("w_rnn", [D, D])
    w_att = din("w_att", [512, D])
    w_out = din("w_out", [D, D])
    w_up = din("w_up", [D, 8192])
    w_dn = din("w_dn", [8192, D])
    wg_src = din("wg", [2, 3, 16, 128, 128])
    prm = din("prm", [128, 23 * 16])
    gvec = din("gvec", [128, 2 * 16])
    gfin = din("gfin", [1, D])
    cst = din("cst", [TO, 128])
    msk = din("msk", [3, 128, 256])
    flg = din("flg", [128, 2])

    XN = dscr("XN", [16, 128, TO], BF16)
    XNo = dscr("XNo", [16, 128, TTK], BF16)
    XR = dscr("XR", [16, 128, T + 1024], F32)
    XRo = dscr("XRo", [16, 128, TTK], F32)
    GG = dscr("GG", [16, 128, T], F32)
    HG = dscr("HG", [16, 128, T], BF16)
    QT = dscr("QT", [12, 128, T], BF16)
    KT = dscr("KT", [12, 128, TO], BF16)
    VV = dscr("VV", [TO, 1536], BF16)
    AT = dscr("AT", [4, 128, T], BF16)
    WIG = dscr("WIG", [32, 128, 16, 128], BF16)
    WRN = dscr("WRN", [16, 128, 16, 128], BF16)
    WAT = dscr("WAT", [16, 128, 4, 128], BF16)
    WOU = dscr("WOU", [4, 128, 16, 512], BF16)
    WUP = dscr("WUP", [16, 128, 16, 512], BF16)
    WDN = dscr("WDN", [4, 128, 64, 512], BF16)

    AW = 52224
    with contextlib.ExitStack() as st:
        arena_t = st.enter_context(nc.sbuf_tensor("arena", [128, AW], F32))
        banks = [st.enter_context(nc.psum_tensor("pb%d" % i, [128, 512], F32)) for i in range(8)]
        A = Arena(arena_t, AW)
        P = Prog(nc)

        def bank(i):
            return banks[i][:]

        def bankb(i):
            return banks[i][:].bitcast(BF16)

        ident32 = A.f32(128)
        ident = A.bf16(128)
        ones = A.bf16(128)
        mask32 = A.f32(3 * 256)
        maskb = A.bf16(3 * 256)
        tiny = A.f32(4)
        flags = A.f32(2)
        soth = A.f32(16)
        initf = A.f32(16)
        initb = A.f32(16)
        gv = A.f32(32)
        A.set_base()

        P.add("pool", lambda e: e.memset(ident32, 1.0), writes=["ident32"])
        P.add("pool", lambda e: e.affine_select(out=ident32, in_=ident32, pattern=[[-1, 128]], compare_op=ALU.is_equal,
                                                fill=0.0, base=0, channel_multiplier=1), reads=["ident32"], writes=["ident32"])
        op_copy(P, "pool", ident, ident32, ["ident32"], ["ident"])
        P.add("pool", lambda e: e.memset(ones, 1.0), writes=["ones"])
        P.dma("sp", mask32.rearrange("p (m q) -> p m q", m=3), msk.rearrange("m p q -> p m q"), writes=["mask32"])
        op_copy(P, "pool", maskb, mask32, ["mask32"], ["maskb"])
        P.dma("sp", flags, flg, writes=["flags"])
        P.dma("sp", gv, gvec, writes=["gv"])
        P.add("pool", lambda e: e.memset(soth, 0.0), writes=["soth"])

        def phase_p1a():
            A.reset()
            gfull = A.f32(16 * 128).rearrange("p (k c) -> p k c", k=16)
            xs32 = [A.f32(D) for _ in range(8)]
            junk = A.bf16(D)
            xsb = [A.bf16(D) for _ in range(2)]
            xnTs = [A.bf16(16 * 512).rearrange("p (k t) -> p k t", k=16) for _ in range(2)]
            ssq = [A.f32(4) for _ in range(2)]
            ms = [A.f32(4) for _ in range(2)]
            rstd = [A.f32(4) for _ in range(2)]
            for k in range(16):
                op_copy(P, "pool", gfull[:, k, :], gv[:, k:k + 1].to_broadcast([128, 128]), ["gv"], [("gfull", k)])
            gkeys = [("gfull", k) for k in range(16)]
            cnt = 0
            for (src, ntile, dst, dname) in ((xo, NTO, XN, "XN"), (xt, NTT, XNo, "XNo")):
                for i in range(ntile):
                    par = cnt % 2
                    cnt += 1
                    for s in range(4):
                        b = xs32[par * 4 + s]
                        P.dma("sp", b, src[(i * 4 + s) * 128:(i * 4 + s + 1) * 128, :], writes=[("xs32", par, s)])
                        op_act(P, junk, b, AF.Square, [("xs32", par, s)], ["junk", ("ssq", par, s)],
                               accum=ssq[par][:, s:s + 1])
                    op_ts(P, "dve", ms[par], ssq[par], 1.0 / D, EPS, ALU.mult, ALU.add,
                          [("ssq", par, s) for s in range(4)], [("ms", par)])
                    op_act(P, ms[par], ms[par], AF.Sqrt, [("ms", par)], [("ms", par)])
                    P.add("dve", (lambda o, i_: (lambda e: e.reciprocal(out=o, in_=i_)))(rstd[par], ms[par]),
                          [("ms", par)], [("rstd", par)])
                    xn = xnTs[par]
                    for s in range(4):
                        sb_ = xsb[s % 2]
                        op_ts(P, "dve", sb_, xs32[par * 4 + s], rstd[par][:, s:s + 1], None, ALU.mult, None,
                              [("xs32", par, s), ("rstd", par)], [("xsb", s % 2)])
                        for h in range(2):
                            bk = 4 + h
                            op_tr(P, [(bankb(bk)[:, j * 128:(j + 1) * 128], sb_[:, (h * 8 + j) * 128:(h * 8 + j + 1) * 128])
                                      for j in range(8)], ident, [("xsb", s % 2), "ident"], [("bank", bk)])
                            op_tt(P, "dve", xn[:, h * 8:(h + 1) * 8, s * 128:(s + 1) * 128],
                                  bankb(bk).rearrange("p (k c) -> p k c", k=8), gfull[:, h * 8:(h + 1) * 8, :], ALU.mult,
                                  [("bank", bk)] + gkeys, [("xnT", par, s, h)])
                    P.dma("pool", dst[:, :, i * 512:(i + 1) * 512].rearrange("k p t -> p k t"), xn,
                          reads=[("xnT", par, s, h) for s in range(4) for h in range(2)], writes=[(dname, i)])

        def load_resident(Wres, col0, ncols, stage, tag):
            i = 0
            engs = ("act", "dve", "pool")
            for cb in range(ncols // 512):
                for kh in range(2):
                    sg = stage[i % 2].rearrange("p (k c) -> p k c", k=8)
                    P.dma("sp", sg, w_in[kh * 1024:(kh + 1) * 1024, col0 + cb * 512: col0 + (cb + 1) * 512]
                          .rearrange("(k p) c -> p k c", p=128), writes=[("wstage", i % 2)])
                    op_copy(P, engs[i % 3], Wres[:, kh * 8:(kh + 1) * 8, cb * 512:(cb + 1) * 512], sg,
                            [("wstage", i % 2)], [("wres", tag, cb, kh)])
                    i += 1
            return [("wres", tag, cb, kh) for cb in range(ncols // 512) for kh in range(2)]

        def phase_g_formA(tag, col0, tiles_list, evac):
            A.reset()
            Wres = A.bf16(16 * 2048).rearrange("p (k c) -> p k c", k=16)
            stage = [A.f32(8 * 512) for _ in range(2)]
            xb = [A.bf16(16 * 512).rearrange("p (k t) -> p k t", k=16) for _ in range(2)]
            wkeys = load_resident(Wres, col0, 2048, stage, tag)
            state = evac(None, None, None, None, init=True)
            wcg = make_wc() if tag == "xr" else None
            bk = 0

            def load_tile(ti):
                srcT, sname, i = tiles_list[ti]
                P.dma("sp", xb[ti % 2], srcT[:, :, i * 512:(i + 1) * 512].rearrange("k p t -> p k t"),
                      reads=[(sname, i)], writes=[("xb", ti % 2)])

            load_tile(0)
            for ti, (srcT, sname, i) in enumerate(tiles_list):
                x_ = xb[ti % 2]
                for oc in range(16):
                    b = bk % 4
                    bk += 1
                    op_mm(P, bank(b), [(Wres[:, k, oc * 128:(oc + 1) * 128], x_[:, k, :]) for k in range(16)],
                          [("xb", ti % 2)] + wkeys, [("bank", b)])
                    evac(b, oc, ti, (sname, i), state=state)
                    if oc == 0 and ti + 1 < len(tiles_list):
                        load_tile(ti + 1)
                    if wcg is not None and oc in (2, 7, 12):
                        next(wcg, None)
            if wcg is not None:
                for _ in wcg:
                    pass

        def make_evac_xr():
            def evac(b, oc, ti, info, init=False, state=None):
                if init:
                    return dict(stg=[A.f32(4 * 512).rearrange("p (k t) -> p k t", k=4) for _ in range(2)], n=0)
                sname, i = info
                g = state["n"] // 4
                stg = state["stg"][g % 2]
                j = oc % 4
                op_copy(P, "act" if oc % 2 == 0 else "dve", stg[:, j, :], bank(b), [("bank", b)], [("stg", g % 2, j)])
                state["n"] += 1
                if j == 3:
                    if sname == "XN":
                        dst = XR[oc - 3:oc + 1, :, (i - 1) * 512:i * 512]
                        wk = ("XR", oc // 4, i)
                    else:
                        dst = XRo[oc - 3:oc + 1, :, i * 512:(i + 1) * 512]
                        wk = ("XRo", oc // 4, i)
                    P.dma("pool", dst.rearrange("k p t -> p k t"), stg, reads=[("stg", g % 2, jj) for jj in range(4)],
                          writes=[wk])
            return evac

        def make_evac_gelu():
            def evac(b, oc, ti, info, init=False, state=None):
                if init:
                    return dict(stg=[A.f32(4 * 512).rearrange("p (k t) -> p k t", k=4) for _ in range(2)], n=0,
                                t1=[A.f32(512) for _ in range(3)], t2=[A.f32(512) for _ in range(3)])
                sname, i = info
                n = state["n"]
                g = n // 4
                stg = state["stg"][g % 2]
                j = oc % 4
                r3 = n % 3
                t1 = state["t1"][r3]
                t2 = state["t2"][r3]
                op_act(P, t1, bank(b), AF.Square, [("bank", b)], [("t1", r3)])
                op_act(P, t2, bank(b), AF.Copy, [("bank", b)], [("t2", r3)], scale=0.5)
                op_ts(P, "dve", t1, t1, 0.044715, 1.0, ALU.mult, ALU.add, [("t1", r3)], [("t1", r3)])
                op_tt(P, "pool", t1, t1, t2, ALU.mult, [("t1", r3), ("t2", r3)], [("t1", r3)])
                op_act(P, t1, t1, AF.Tanh, [("t1", r3)], [("t1", r3)], scale=1.5957691216057308)
                op_stt(P, stg[:, j, :], t1, 1.0, t2, ALU.add, ALU.mult, [("t1", r3), ("t2", r3)], [("stg", g % 2, j)])
                state["n"] += 1
                if j == 3:
                    P.dma("pool", GG[oc - 3:oc + 1, :, (i - 2) * 512:(i - 1) * 512].rearrange("k p t -> p k t"), stg,
                          reads=[("stg", g % 2, jj) for jj in range(4)], writes=[("GG", oc // 4, i - 2)])
            return evac

        def phase_g_formB(tag, col0, tiles, kind):
            A.reset()
            Wres = A.bf16(16 * 1536).rearrange("p (k c) -> p k c", k=16)
            stage = [A.f32(8 * 512) for _ in range(2)]
            xb = [A.bf16(16 * 512).rearrange("p (k t) -> p k t", k=16) for _ in range(2)]
            wkeys = load_resident(Wres, col0, 1536, stage, tag)
            if kind in ("q", "k"):
                cs = [A.f32(4 * 128).rearrange("p (s x) -> p s x", s=4) for _ in range(2)]
                qb = [[A.bf16(512) for _ in range(12)] for _ in range(2)]
                rt = [[A.f32(64).rearrange("p (h e) -> p h e", h=4) for _ in range(4)] for _ in range(2)]
                outs = [A.bf16(12 * 512).rearrange("p (h t) -> p h t", h=12) for _ in range(2)]
            else:
                outs = [A.bf16(4 * 1536).rearrange("p (s c) -> p s c", s=4) for _ in range(2)]
            bk = 0
            n = 0
            for ti, i in enumerate(tiles):
                x_ = xb[ti % 2]
                P.dma("sp", x_, XN[:, :, i * 512:(i + 1) * 512].rearrange("k p t -> p k t"),
                      reads=[("XN", i)], writes=[("xb", ti % 2)])
                ot = outs[ti % 2]
                okeys = []
                if kind in ("q", "k"):
                    c_ = cs[ti % 2]
                    P.dma("sp", c_, cst[i * 512:(i + 1) * 512, :].rearrange("(s p) x -> p s x", p=128), writes=[("cs", ti % 2)])
                    if kind == "q":
                        cf = c_.rearrange("p s x -> p (s x)")
                        op_ts(P, "pool", cf, cf, QSCALE, None, ALU.mult, None, [("cs", ti % 2)], [("cs", ti % 2)])
                for cb in range(3):
                    for s in range(4):
                        b = bk % 4
                        bk += 1
                        op_mm(P, bank(b), [(x_[:, k, s * 128:(s + 1) * 128], Wres[:, k, cb * 512:(cb + 1) * 512]) for k in range(16)],
                              [("xb", ti % 2)] + wkeys, [("bank", b)])
                        if kind == "v":
                            kk = ("outs", ti % 2, cb, s)
                            op_copy(P, "act" if n % 2 == 0 else "dve", ot[:, s, cb * 512:(cb + 1) * 512], bank(b), [("bank", b)], [kk])
                            okeys.append(kk)
                            n += 1
                            continue
                        r2 = n % 2
                        n += 1
                        qi = cb * 4 + s
                        qb_ = qb[ti % 2][qi]
                        qk = ("qb", ti % 2, qi)
                        op_act(P, qb_, bank(b), AF.Copy, [("bank", b)], [qk], scale=(QSCALE if kind == "q" else 1.0))
                        p4 = bank(b).rearrange("p (h e) -> p h e", h=4)
                        qb4 = qb_.rearrange("p (h e) -> p h e", h=4)
                        cos4 = c_[:, s, 0:64].rearrange("p (h e) -> p h e", h=4)
                        sin4 = c_[:, s, 64:128].rearrange("p (h e) -> p h e", h=4)
                        t1 = p4[:, :, 0:16]
                        t2 = p4[:, :, 16:32]
                        ra, rb, rc, rd = rt[r2]
                        rk = [("rt", r2, z) for z in range(4)]
                        op_tt(P, "dve", ra, t1, cos4, ALU.mult, [("bank", b), ("cs", ti % 2)], [rk[0]])
                        op_tt(P, "dve", rb, t2, sin4, ALU.mult, [("bank", b), ("cs", ti % 2)], [rk[1]])
                        op_tt(P, "dve", rc, t2, cos4, ALU.mult, [("bank", b), ("cs", ti % 2)], [rk[2]])
                        op_tt(P, "dve", rd, t1, sin4, ALU.mult, [("bank", b), ("cs", ti % 2)], [rk[3]])
                        op_tt(P, "dve", qb4[:, :, 0:16], ra, rb, ALU.subtract, [rk[0], rk[1], qk], [qk])
                        op_tt(P, "dve", qb4[:, :, 16:32], rc, rd, ALU.add, [rk[2], rk[3], qk], [qk])
                if kind in ("q", "k"):
                    for g6 in range(6):
                        tb = 4 + (g6 % 2)
                        items = []
                        rkeys = ["ident"]
                        for u in range(2):
                            qi = g6 * 2 + u
                            rkeys.append(("qb", ti % 2, qi))
                            for h in range(4):
                                items.append((bankb(tb)[:, (u * 4 + h) * 128:(u * 4 + h + 1) * 128],
                                              qb[ti % 2][qi][:, h * 128:(h + 1) * 128]))
                        op_tr(P, items, ident, rkeys, [("bank", tb)])
                        for u in range(2):
                            qi = g6 * 2 + u
                            cb, s = qi // 4, qi % 4
                            kk = ("outs", ti % 2, cb, s)
                            op_copy(P, "act" if u == 0 else "dve", ot[:, cb * 4:(cb + 1) * 4, s * 128:(s + 1) * 128],
                                    bankb(tb)[:, u * 512:(u + 1) * 512].rearrange("p (h t) -> p h t", h=4), [("bank", tb)], [kk])
                            okeys.append(kk)
                if kind == "q":
                    P.dma("pool", QT[:, :, (i - 2) * 512:(i - 1) * 512].rearrange("h p t -> p h t"), ot, reads=okeys, writes=[("QT", i - 2)])
                elif kind == "k":
                    P.dma("pool", KT[:, :, i * 512:(i + 1) * 512].rearrange("h p t -> p h t"), ot, reads=okeys, writes=[("KT", i)])
                else:
                    P.dma("pool", VV[i * 512:(i + 1) * 512, :].rearrange("(s p) c -> p s c", p=128), ot, reads=okeys, writes=[("VV", i)])

        def make_wc():
            NS = 2
            stage = [A.f32(8 * 512) for _ in range(NS)]
            bst = [A.bf16(8 * 512) for _ in range(NS)]
            engs = ("pool", "act", "dve")
            cnt = [0]

            def conv(src, K, c0, ncols, dst, PW, dname):
                kcs = K // 128
                for cb in range(ncols // 512):
                    for kh in range((kcs + 7) // 8):
                        nk = min(8, kcs - kh * 8)
                        r = cnt[0] % NS
                        cnt[0] += 1
                        sg = stage[r][:, 0:nk * 512].rearrange("p (k c) -> p k c", k=nk)
                        P.dma("sp", sg, src[kh * 1024:kh * 1024 + nk * 128, c0 + cb * 512:c0 + (cb + 1) * 512]
                              .rearrange("(k p) c -> p k c", p=128), writes=[("wcs", r)])
                        npn = 512 // PW
                        bo = bst[r][:, 0:nk * 512]
                        op_copy(P, engs[cnt[0] % 3], bo.rearrange("p (n k c) -> p k n c", n=npn, k=nk),
                                sg.rearrange("p k (n c) -> p k n c", n=npn), [("wcs", r)], [("wcb", r)])
                        P.dma("pool", dst[cb * npn:(cb + 1) * npn, :, kh * 8:kh * 8 + nk, :].rearrange("n p k c -> p n k c"),
                              bo.rearrange("p (n k c) -> p n k c", n=npn, k=nk), reads=[("wcb", r)], writes=[(dname, cb, kh)])
                        yield

            def gen():
                yield from conv(w_in, 2048, 8704, 4096, WIG, 128, "WIG")
                yield from conv(w_rnn, 2048, 0, 2048, WRN, 128, "WRN")
                yield from conv(w_att, 512, 0, 2048, WAT, 128, "WAT")
                yield from conv(w_out, 2048, 0, 2048, WOU, 512, "WOU")
                yield from conv(w_up, 2048, 0, 8192, WUP, 512, "WUP")
                yield from conv(w_dn, 8192, 0, 2048, WDN, 512, "WDN")
            return gen()

        def phase_p2():
            A.reset()
            NB = T // 1024
            pr = A.f32(23 * 16)
            prv = pr.rearrange("p (n c) -> p n c", n=23)
            hba = A.f32(48).rearrange("p (n c) -> p n c", n=3)
            hbi = A.f32(48).rearrange("p (n c) -> p n c", n=3)
            ncs = A.f32(48).rearrange("p (n c) -> p n c", n=3)
            hncs = A.f32(48).rearrange("p (n c) -> p n c", n=3)
            NSET = 4
            wgs = [A.f32(128 * 6).rearrange("p (g d o) -> p g d o", g=2, d=3) for _ in range(2)]
            wgbs = [A.bf16(128 * 6).rearrange("p (g d o) -> p g d o", g=2, d=3) for _ in range(2)]
            xr = A.f32(T + 8)
            xc = A.f32(T)
            xcb = A.bf16(T)
            hf = A.f32(T)
            dgs = [[A.f32(128) for _ in range(5)] for _ in range(2)]
            Ab = [A.f32(1024) for _ in range(NSET)]
            Bb = [A.f32(1024) for _ in range(NSET)]
            Cb = [A.f32(1024) for _ in range(NSET)]
            ggb = [A.f32(1024) for _ in range(2)]
            HBb = [A.f32(1024) for _ in range(2)]
            OUb = [A.bf16(1024) for _ in range(2)]

            P.dma("sp", pr, prm, writes=["pr"])
            op_ts(P, "dve", hba.rearrange("p n c -> p (n c)"), pr[:, 160:208], 0.5, None, ALU.mult, None, ["pr"], ["hba"])
            op_ts(P, "dve", hbi.rearrange("p n c -> p (n c)"), pr[:, 208:256], 0.5, None, ALU.mult, None, ["pr"], ["hbi"])
            ncf = ncs.rearrange("p n c -> p (n c)")
            op_act(P, ncf, pr[:, 256:304], AF.Exp, ["pr"], ["ncs"], scale=-1.0)
            op_act(P, ncf, ncf, AF.Ln, ["ncs"], ["ncs"], bias=1.0)
            op_ts(P, "dve", ncf, ncf, -8.0, None, ALU.mult, None, ["ncs"], ["ncs"])
            op_ts(P, "dve", hncs.rearrange("p n c -> p (n c)"), ncf, 0.5, None, ALU.mult, None, ["ncs"], ["hncs"])
            def load_gates(c):
                sl = c % 2
                P.dma("sp", wgs[sl], wg_src[:, :, c].rearrange("g d i o -> i g d o"), writes=[("wgs", sl)])
                op_copy(P, "pool", wgbs[sl], wgs[sl], [("wgs", sl)], [("wgb", sl)])
                return wgbs[sl], ("wgb", sl)

            pkeys = ["hba", "hbi", "ncs", "hncs", "pr"]
            bkc = [0]
            blk = [0]
            outc = [0]

            def run_dir(c, d, reverse, ntok, init_ap, init_keys, emit_out, wgb, wgk, conv_fn=None, post_conv=None):
                nb = ntok // 1024
                order = list(range(nb - 1, -1, -1) if reverse else range(nb))
                prev = None
                if conv_fn is not None:
                    conv_fn(c, order[0])
                    if nb == 1 and post_conv is not None:
                        post_conv()
                for p0 in range(0, nb, 2):
                    info = []
                    for bi in order[p0:p0 + 2]:
                        r = blk[0] % NSET
                        blk[0] += 1
                        Ax, Bx, Cx = Ab[r], Bb[r], Cb[r]
                        if conv_fn is not None:
                            nxt = order.index(bi) + 1
                            if nxt < nb:
                                conv_fn(c, order[nxt])
                                if nxt == nb - 1 and post_conv is not None:
                                    post_conv()
                        for tl in range(2):
                            t0 = bi * 1024 + tl * 512
                            ba = 4 + bkc[0] % 4
                            bb = 4 + (bkc[0] + 1) % 4
                            bkc[0] += 2
                            op_mm(P, bank(ba), [(wgb[:, 0, d, :], xcb[:, t0:t0 + 512])], [wgk, ("xcb", bi, tl)], [("bank", ba)])
                            op_mm(P, bank(bb), [(wgb[:, 1, d, :], xcb[:, t0:t0 + 512])], [wgk, ("xcb", bi, tl)], [("bank", bb)])
                            op_act(P, Ax[:, tl * 512:(tl + 1) * 512], bank(ba), AF.Tanh, [("bank", ba)] + pkeys, [("A", r, tl)],
                                   scale=0.5, bias=hba[:, d, c:c + 1])
                            op_act(P, Bx[:, tl * 512:(tl + 1) * 512], bank(bb), AF.Tanh, [("bank", bb)] + pkeys, [("B", r, tl)],
                                   scale=0.5, bias=hbi[:, d, c:c + 1])
                        ak = [("A", r, 0), ("A", r, 1)]
                        bkeys = [("B", r, 0), ("B", r, 1)]
                        if reverse:
                            op_act(P, Ax, Ax, AF.Exp, ak + pkeys, ak, scale=hncs[:, d, c:c + 1], bias=hncs[:, d, c:c + 1])
                            op_tt(P, "dve", Cx, Ax, Ax, ALU.mult, ak, [("C", r)])
                        else:
                            op_act(P, Cx, Ax, AF.Exp, ak + pkeys, [("C", r)], scale=ncs[:, d, c:c + 1], bias=ncs[:, d, c:c + 1])
                            op_act(P, Ax, Ax, AF.Exp, ak + pkeys, ak, scale=hncs[:, d, c:c + 1], bias=hncs[:, d, c:c + 1])
                        info.append((bi, r, Ax, Bx, Cx, ak, bkeys))
                    for (bi, r, Ax, Bx, Cx, ak, bkeys) in info:
                        op_act(P, Cx, Cx, AF.Sqrt, [("C", r)], [("C", r)], scale=-0.25, bias=0.25)
                    for (bi, r, Ax, Bx, Cx, ak, bkeys) in info:
                        op_stt(P, Bx, Bx, 1.0, xc[:, bi * 1024:(bi + 1) * 1024], ALU.add, ALU.mult,
                               bkeys + [("xc", bi, 0), ("xc", bi, 1)], bkeys)
                        op_tt(P, "dve", Bx, Bx, Cx, ALU.mult, bkeys + [("C", r)], bkeys)
                        if prev is None:
                            ini, inik = init_ap, init_keys
                        else:
                            ini, inik = prev
                        if not reverse:
                            o = hf[:, bi * 1024:(bi + 1) * 1024]
                            P.add("dve", (lambda o_, a_, b_, i_: (lambda e: e.tensor_tensor_scan(
                                out=o_, data0=a_, data1=b_, initial=i_, op0=ALU.mult, op1=ALU.add)))(o, Ax, Bx, ini),
                                ak + bkeys + inik, [("hf", bi)])
                            prev = (hf[:, (bi + 1) * 1024 - 1:(bi + 1) * 1024], [("hf", bi)])
                        else:
                            ro = outc[0] % 2
                            outc[0] += 1
                            hbuf = HBb[ro]
                            P.add("dve", (lambda o_, a_, b_, i_: (lambda e: e.tensor_tensor_scan(
                                out=o_, data0=a_, data1=b_, initial=i_, op0=ALU.mult, op1=ALU.add)))(
                                hbuf[:, ::-1], Ax[:, ::-1], Bx[:, ::-1], ini),
                                ak + bkeys + inik, [("hb", ro)])
                            prev = (hbuf[:, 0:1], [("hb", ro)])
                            emit_out(bi, r, ro, hbuf, Cx)

            def load_own(c):
                P.dma("sp", xr[:, 0:T + 4], XR[c, :, 510:510 + T + 4],
                      reads=[("XR", c // 4, i) for i in range(1, NTO - 1)], writes=["xr"])

            def load_oth(c):
                P.add("pool", lambda e: e.memset(xr[:, 0:2], 0.0), writes=["xr"])
                P.dma("sp", xr[:, 2:T + 6], XRo[c, :, 0:T + 4], reads=[("XRo", c // 4, i) for i in range(NTT)], writes=["xr"])

            cvb = [0]

            def make_diags(c, ntap, w0):
                par = c % 2
                for j in range(ntap):
                    op_ts(P, "dve", dgs[par][j], ident32, prv[:, w0 + j, c:c + 1], None, ALU.mult, None,
                          ["ident32", "pr"], [("dg", par, j)])

            def conv_blk(c, bi, ntap, w0):
                par = c % 2
                for tl in range(2):
                    o = bi * 1024 + tl * 512
                    cb_ = cvb[0] % 4
                    cvb[0] += 1
                    op_mm(P, bank(cb_), [(dgs[par][j], xr[:, o + j:o + j + 512]) for j in range(ntap)],
                          ["xr"] + [("dg", par, j) for j in range(ntap)], [("bank", cb_)])
                    op_ts(P, "dve", xc[:, o:o + 512], bank(cb_), prv[:, 4, c:c + 1], None, ALU.add, None,
                          [("bank", cb_), "pr"], [("xc", bi, tl)])
                    op_act(P, xcb[:, o:o + 512], xc[:, o:o + 512], AF.Copy, [("xc", bi, tl)], [("xcb", bi, tl)])

            def conv_own(c, bi):
                conv_blk(c, bi, 4, 0)

            def conv_oth(c, bi):
                conv_blk(c, bi, 5, 5)

            prepped = {}

            def prep_oth(c):
                prepped[("o", c)] = load_gates(c)
                make_diags(c, 5, 5)
                load_oth(c)

            def prep_own(c):
                prepped[("w", c)] = load_gates(c)
                make_diags(c, 4, 0)
                load_own(c)

            prep_oth(0)
            for c in range(16):
                wgb_, wgk_ = prepped[("o", c)]
                nxt_fn = (lambda c=c: prep_oth(c + 1)) if c < 15 else (lambda: prep_own(0))
                run_dir(c, 2, False, T, 0.0, [], None, wgb_, wgk_, conv_fn=conv_oth, post_conv=nxt_fn)
                op_copy(P, "dve", soth[:, c:c + 1], hf[:, T - 1:T], [("hf", NB - 1)], ["soth"])
            op_ts(P, "dve", initf, soth, flags[:, 0:1], None, ALU.mult, None, ["soth", "flags"], ["initf"])
            op_ts(P, "dve", initb, soth, flags[:, 1:2], None, ALU.mult, None, ["soth", "flags"], ["initb"])

            for c in range(16):
                wgb_, wgk_ = prepped[("w", c)]
                nxt_fn = (lambda c=c: prep_own(c + 1)) if c < 15 else None
                run_dir(c, 0, False, T, initf[:, c:c + 1], ["initf"], None, wgb_, wgk_, conv_fn=conv_own, post_conv=nxt_fn)

                def emit_out(bi, r, ro, hbuf, Cx, c=c):
                    g_ = ggb[ro]
                    P.dma("sp", g_, GG[c, :, bi * 1024:(bi + 1) * 1024],
                          reads=[("GG", c // 4, i) for i in range(bi * 2, bi * 2 + 2)], writes=[("gg", ro)])
                    op_tt(P, "dve", Cx, hbuf, hf[:, bi * 1024:(bi + 1) * 1024], ALU.add, [("hb", ro), ("hf", bi), ("C", r)], [("C", r)])
                    op_tt(P, "pool", OUb[ro], Cx, g_, ALU.mult, [("C", r), ("gg", ro)], [("ou", ro)])
                    P.dma("pool", HG[c, :, bi * 1024:(bi + 1) * 1024], OUb[ro], reads=[("ou", ro)], writes=[("HG", c, bi)])

                run_dir(c, 1, True, T, initb[:, c:c + 1], ["initb"], emit_out, wgb_, wgk_)

        def phase_p3():
            A.reset()
            NST = T // 2048
            Uacc = A.f32(4 * 2048).rearrange("p (j t) -> p j t", j=4)
            Dacc = A.f32(4 * 2048).rearrange("p (j t) -> p j t", j=4)
            qbuf = A.bf16(4 * 2048).rearrange("p (j t) -> p j t", j=4)
            kbuf = A.bf16(4 * 4096).rearrange("p (j t) -> p j t", j=4)
            vb = [A.bf16(512) for _ in range(3)]
            pt = [A.bf16(4 * 256).rearrange("p (j q) -> p j q", j=4) for _ in range(3)]
            atb = A.bf16(4 * 2048).rearrange("p (j t) -> p j t", j=4)
            mk = maskb.rearrange("p (m q) -> p m q", m=3)
            vcnt = [0]
            scnt = [0]
            for stile in range(NST):
                for g, d in enumerate((1, 4, 16)):
                    W = 2048 + 128 * d
                    B0 = 1024 + stile * 2048 - 64 * d
                    P.dma("sp", qbuf, QT[4 * g:4 * g + 4, :, stile * 2048:(stile + 1) * 2048].rearrange("h p t -> p h t"),
                          reads=[("QT", i) for i in range(stile * 4, stile * 4 + 4)], writes=["qbuf"])
                    P.dma("sp", kbuf[:, :, 0:W], KT[4 * g:4 * g + 4, :, B0:B0 + W].rearrange("h p t -> p h t"),
                          reads=[("KT", i) for i in range(NTO)], writes=["kbuf"])
                    nq = 16 // d
                    for r in range(d):
                        slots = {}

                        def do_chunk(m, r=r, d=d, g=g, stile=stile, nq=nq, B0=B0):
                            sl = vcnt[0] % 3
                            vcnt[0] += 1
                            row0 = B0 + r + 128 * d * m
                            P.dma("sp", vb[sl], VV[ss(row0, 128, d), 512 * g:512 * (g + 1)],
                                  reads=[("VV", i) for i in range(NTO)], writes=[("vb", sl)])
                            lo = 128 if m == 0 else 0
                            hi = 128 if m == nq else 256
                            if stile == 0 and m == 0:
                                mi = 1
                            elif stile == NST - 1 and m == nq:
                                mi = 2
                            else:
                                mi = 0
                            ptile = pt[sl]
                            for j in range(4):
                                sb_ = 4 + (scnt[0] % 4)
                                scnt[0] += 1
                                kc0 = r + 128 * d * m
                                qc0 = r + 128 * d * (m - 1) + lo * d
                                nqq = hi - lo
                                op_mm(P, bank(sb_)[:, lo:hi],
                                      [(kbuf[:, j, ss(kc0, 128, d)], qbuf[:, j, ss(qc0, nqq, d)])],
                                      ["kbuf", "qbuf"], [("bank", sb_)])
                                op_act(P, ptile[:, j, lo:hi], bank(sb_)[:, lo:hi], AF.Exp, [("bank", sb_)], [("pt", sl, j)])
                                op_tt(P, "pool", ptile[:, j, lo:hi], ptile[:, j, lo:hi], mk[:, mi, lo:hi], ALU.mult,
                                      [("pt", sl, j), "maskb"], [("pt", sl, j)])
                            slots[m] = sl

                        do_chunk(0)
                        for i in range(nq):
                            do_chunk(i + 1)
                            s0, s1 = slots[i], slots[i + 1]
                            pr_ = [("pt", s0, j) for j in range(4)] + [("pt", s1, j) for j in range(4)]

                            pb_ = 0 if (i % 2 == 0) else 2
                            db_ = pb_ + 1

                            def pv(e, s0=s0, s1=s1, pb_=pb_):
                                ins = None
                                for j in range(4):
                                    e.matmul(bank(pb_)[:, j * 128:(j + 1) * 128], lhsT=vb[s0][:, j * 128:(j + 1) * 128],
                                             rhs=pt[s0][:, j, 128:256], start=True, stop=False)
                                    ins = e.matmul(bank(pb_)[:, j * 128:(j + 1) * 128], lhsT=vb[s1][:, j * 128:(j + 1) * 128],
                                                   rhs=pt[s1][:, j, 0:128], start=False, stop=True)
                                return ins

                            def dn(e, s0=s0, s1=s1, db_=db_):
                                ins = None
                                for j in range(4):
                                    e.matmul(bank(db_)[:, j * 128:(j + 1) * 128], lhsT=ones, rhs=pt[s0][:, j, 128:256],
                                             start=True, stop=False)
                                    ins = e.matmul(bank(db_)[:, j * 128:(j + 1) * 128], lhsT=ones, rhs=pt[s1][:, j, 0:128],
                                                   start=False, stop=True)
                                return ins

                            P.add("pe", pv, pr_ + [("vb", s0), ("vb", s1)], [("bank", pb_)])
                            P.add("pe", dn, pr_ + ["ones"], [("bank", db_)])
                            c0 = r + 128 * d * i
                            usl = Uacc[:, :, ss(c0, 128, d)]
                            dsl = Dacc[:, :, ss(c0, 128, d)]
                            b6 = bank(pb_).rearrange("p (j q) -> p j q", j=4)
                            b7 = bank(db_).rearrange("p (j q) -> p j q", j=4)
                            if g == 0:
                                op_copy(P, "dve", usl, b6, [("bank", pb_)], ["Uacc"])
                                op_copy(P, "act", dsl, b7, [("bank", db_)], ["Dacc"])
                            else:
                                op_tt(P, "dve", usl, usl, b6, ALU.add, [("bank", pb_), "Uacc"], ["Uacc"])
                                op_tt(P, "dve", dsl, dsl, b7, ALU.add, [("bank", db_), "Dacc"], ["Dacc"])
                Df = Dacc.rearrange("p j t -> p (j t)")
                Uf = Uacc.rearrange("p j t -> p (j t)")
                P.add("dve", lambda e: e.reciprocal(out=Df, in_=Df), ["Dacc"], ["Dacc"])
                op_tt(P, "dve", atb.rearrange("p j t -> p (j t)"), Uf, Df, ALU.mult, ["Uacc", "Dacc"], ["atb"])
                P.dma("pool", AT[:, :, stile * 2048:(stile + 1) * 2048].rearrange("j p t -> p j t"), atb,
                      reads=["atb"], writes=[("AT", stile)])

        def phase_p4():
            A.reset()
            gF = A.f32(D)
            R1 = A.off
            xnT = A.bf16(16 * 512).rearrange("p (k t) -> p k t", k=16)
            hgT = A.bf16(16 * 512).rearrange("p (k t) -> p k t", k=16)
            atT = A.bf16(4 * 512).rearrange("p (k t) -> p k t", k=4)
            mxb = A.bf16(16 * 512).rearrange("p (k t) -> p k t", k=16)
            A.off = R1
            HT = A.bf16(64 * 512).rearrange("p (k t) -> p k t", k=64)
            H1 = A.f32(4 * D).rearrange("p (s c) -> p s c", s=4)
            h1nT = A.bf16(16 * 512).rearrange("p (k t) -> p k t", k=16)
            hsb = [A.bf16(D) for _ in range(1)]
            junk = A.bf16(D)
            wp = [A.bf16(16 * 512) for _ in range(3)]
            tA = [A.f32(512) for _ in range(2)]
            tB = [A.f32(512) for _ in range(2)]
            rl = [A.f32(512) for _ in range(2)]
            ssq = A.f32(4)
            ms = A.f32(4)
            rstd = A.f32(4)
            ssq2 = A.f32(4)
            ms2 = A.f32(4)
            rstd2 = A.f32(4)
            P.dma("sp", gF, gfin.partition_broadcast(128), writes=["gF"])
            htkeys = [("HT", c) for c in range(64)]
            wpc = [0]
            wqc = [0]
            bkc = [0]

            def load_panel(src_ap, nelem, rkeys):
                sl = wpc[0] % 3
                wpc[0] += 1
                dst = wp[sl][:, 0:nelem]
                P.dma("sp", dst, src_ap, reads=rkeys, writes=[("wp", sl, j) for j in range(4)])
                return sl

            for t in range(NOWN):
                c0 = t * 512
                P.dma("sp", xnT, XN[:, :, (t + 2) * 512:(t + 3) * 512].rearrange("k p t -> p k t"),
                      reads=[("XN", t + 2)], writes=["xnT"] + htkeys)
                P.dma("sp", hgT, HG[:, :, c0:c0 + 512].rearrange("k p t -> p k t"),
                      reads=[("HG", c, t // 2) for c in range(16)], writes=["hgT"] + htkeys)
                P.dma("sp", atT, AT[:, :, c0:c0 + 512].rearrange("k p t -> p k t"),
                      reads=[("AT", t // 4)], writes=["atT"] + htkeys)
                for s in range(4):
                    P.dma("sp", H1[:, s, :], xo[HALO + c0 + s * 128:HALO + c0 + (s + 1) * 128, :], writes=[("H1", s)])
                for c in range(16):
                    sl = wpc[0] % 3
                    wpc[0] += 1
                    wq_ = wp[sl]
                    w_r = wq_[:, 0:2048].rearrange("p (k c) -> p k c", k=16)
                    w_gr = wq_[:, 2048:4096].rearrange("p (k c) -> p k c", k=16)
                    w_ga = wq_[:, 4096:6144].rearrange("p (k c) -> p k c", k=16)
                    w_a = wq_[:, 6144:6656].rearrange("p (k c) -> p k c", k=4)
                    P.dma("sp", w_r, WRN[c], writes=[("wp", sl, 0)])
                    P.dma("sp", w_gr, WIG[c], writes=[("wp", sl, 1)])
                    P.dma("sp", w_ga, WIG[16 + c], writes=[("wp", sl, 2)])
                    P.dma("sp", w_a, WAT[c], writes=[("wp", sl, 3)])
                    r2 = c % 2
                    b0 = 4 * (c % 2)
                    op_mm(P, bank(b0), [(w_r[:, k, :], hgT[:, k, :]) for k in range(16)], [("wp", sl, 0), "hgT"], [("bank", b0)])
                    op_mm(P, bank(b0 + 1), [(w_gr[:, k, :], xnT[:, k, :]) for k in range(16)], [("wp", sl, 1), "xnT"], [("bank", b0 + 1)])
                    op_act(P, tA[r2], bank(b0 + 1), AF.Tanh, [("bank", b0 + 1)], [("tA", r2)], scale=0.5)
                    op_stt(P, tA[r2], tA[r2], 1.0, bank(b0), ALU.add, ALU.mult, [("tA", r2), ("bank", b0)], [("tA", r2)])
                    op_mm(P, bank(b0 + 2), [(w_a[:, k, :], atT[:, k, :]) for k in range(4)], [("wp", sl, 3), "atT"], [("bank", b0 + 2)])
                    op_mm(P, bank(b0 + 3), [(w_ga[:, k, :], xnT[:, k, :]) for k in range(16)], [("wp", sl, 2), "xnT"], [("bank", b0 + 3)])
                    op_act(P, tB[r2], bank(b0 + 3), AF.Tanh, [("bank", b0 + 3)], [("tB", r2)], scale=0.5)
                    op_stt(P, tB[r2], tB[r2], 1.0, bank(b0 + 2), ALU.add, ALU.mult, [("tB", r2), ("bank", b0 + 2)], [("tB", r2)])
                    op_tt(P, "pool", tA[r2], tA[r2], tB[r2], ALU.add, [("tA", r2), ("tB", r2)], [("tA", r2)])
                    op_act(P, mxb[:, c, :], tA[r2], AF.Copy, [("tA", r2)], [("mxb", c)], scale=0.5)
                mxkeys = [("mxb", c) for c in range(16)]
                for cb in range(4):
                    sl = load_panel(WOU[cb].rearrange("p k c -> p (k c)"), 16 * 512, [])
                    wv = wp[sl].rearrange("p (k c) -> p k c", k=16)
                    for s in range(4):
                        b = bkc[0] % 4
                        bkc[0] += 1
                        op_mm(P, bank(b), [(mxb[:, k, s * 128:(s + 1) * 128], wv[:, k, :]) for k in range(16)],
                              mxkeys + [("wp", sl, j) for j in range(4)], [("bank", b)])
                        op_tt(P, "dve", H1[:, s, cb * 512:(cb + 1) * 512], H1[:, s, cb * 512:(cb + 1) * 512], bank(b), ALU.add,
                              [("bank", b), ("H1", s)], [("H1", s)])
                for s in range(4):
                    op_act(P, junk, H1[:, s, :], AF.Square, [("H1", s)], ["junk", ("ssq", s)], accum=ssq[:, s:s + 1])
                op_ts(P, "dve", ms, ssq, 1.0 / D, EPS, ALU.mult, ALU.add, [("ssq", s) for s in range(4)], ["ms"])
                op_act(P, ms, ms, AF.Sqrt, ["ms"], ["ms"])
                P.add("dve", lambda e: e.reciprocal(out=rstd, in_=ms), ["ms"], ["rstd"])
                for s in range(4):
                    sb_ = hsb[0]
                    op_act(P, sb_, H1[:, s, :], AF.Copy, [("H1", s), "rstd"], [("hsb", 0)], scale=rstd[:, s:s + 1])
                    for h in range(2):
                        bk = 4 + h
                        op_tr(P, [(bankb(bk)[:, j * 128:(j + 1) * 128], sb_[:, (h * 8 + j) * 128:(h * 8 + j + 1) * 128])
                                  for j in range(8)], ident, [("hsb", 0), "ident"], [("bank", bk)])
                        for j in range(8):
                            k = h * 8 + j
                            if h == 0:
                                op_ts(P, "dve", h1nT[:, k, s * 128:(s + 1) * 128], bankb(bk)[:, j * 128:(j + 1) * 128],
                                      gv[:, 16 + k:17 + k], None, ALU.mult, None, [("bank", bk), "gv"], [("h1nT", s, h, j)])
                            else:
                                op_act(P, h1nT[:, k, s * 128:(s + 1) * 128], bankb(bk)[:, j * 128:(j + 1) * 128], AF.Copy,
                                       [("bank", bk), "gv"], [("h1nT", s, h, j)], scale=gv[:, 16 + k:17 + k])
                hnkeys = [("h1nT", s, h, j) for s in range(4) for h in range(2) for j in range(8)]
                for pn in range(16):
                    sl = load_panel(WUP[pn].rearrange("p k c -> p (k c)"), 16 * 512, [])
                    wv = wp[sl].rearrange("p (k c) -> p k c", k=16)
                    for oc in range(4):
                        c = pn * 4 + oc
                        b = bkc[0] % 4
                        bkc[0] += 1
                        r3 = c % 2
                        op_mm(P, bank(b), [(wv[:, k, oc * 128:(oc + 1) * 128], h1nT[:, k, :]) for k in range(16)],
                              hnkeys + [("wp", sl, j) for j in range(4)], [("bank", b)])
                        op_act(P, rl[r3], bank(b), AF.Relu, [("bank", b)], [("rl", r3)])
                        op_tt(P, "pool", HT[:, c, :], rl[r3], rl[r3], ALU.mult, [("rl", r3)], [("HT", c)])
                for cb in range(4):
                    sls = []
                    for qd in range(4):
                        sls.append(None)
                    for qd in range(4):
                        sl = load_panel(WDN[cb, :, qd * 16:(qd + 1) * 16, :].rearrange("p k c -> p (k c)"), 16 * 512, [])
                        wv = wp[sl].rearrange("p (k c) -> p k c", k=16)
                        for s in range(4):
                            def grp(e, wv=wv, s=s, qd=qd):
                                ins = None
                                for k in range(16):
                                    ins = e.matmul(bank(s), lhsT=HT[:, qd * 16 + k, s * 128:(s + 1) * 128], rhs=wv[:, k, :],
                                                   start=(qd == 0 and k == 0), stop=(qd == 3 and k == 15))
                                return ins
                            P.add("pe", grp, [("HT", qd * 16 + k) for k in range(16)] + [("wp", sl, j) for j in range(4)], [("bank", s)])
                    for s in range(4):
                        op_tt(P, "dve", H1[:, s, cb * 512:(cb + 1) * 512], H1[:, s, cb * 512:(cb + 1) * 512], bank(s), ALU.add,
                              [("bank", s), ("H1", s)], [("H1", s)])
                    bkc[0] = 0
                for s in range(4):
                    op_act(P, junk, H1[:, s, :], AF.Square, [("H1", s)], ["junk", ("ssq2", s)], accum=ssq2[:, s:s + 1])
                op_ts(P, "dve", ms2, ssq2, 1.0 / D, EPS, ALU.mult, ALU.add, [("ssq2", s) for s in range(4)], ["ms2"])
                op_act(P, ms2, ms2, AF.Sqrt, ["ms2"], ["ms2"])
                P.add("dve", lambda e: e.reciprocal(out=rstd2, in_=ms2), ["ms2"], ["rstd2"])
                for s in range(4):
                    op_stt(P, H1[:, s, :], H1[:, s, :], rstd2[:, s:s + 1], gF, ALU.mult, ALU.mult, [("H1", s), "rstd2", "gF"], [("H1", s)])
                    P.dma("pool", y[c0 + s * 128:c0 + (s + 1) * 128, :], H1[:, s, :], reads=[("H1", s), "OUT"])

        phase_p1a()
        P.barrier(tiny)
        phase_g_formA("xr", 0, [(XN, "XN", i) for i in range(1, NTO - 1)] + [(XNo, "XNo", i) for i in range(NTT)], make_evac_xr())
        P.barrier(tiny)
        phase_g_formA("gr", 2048, [(XN, "XN", i) for i in range(2, NTO - 2)], make_evac_gelu())
        P.barrier(tiny)
        phase_g_formB("q", 4096, list(range(2, NTO - 2)), "q")
        P.barrier(tiny)
        phase_g_formB("k", 4096 + 1536, list(range(NTO)), "k")
        P.barrier(tiny)
        phase_g_formB("v", 4096 + 3072, list(range(NTO)), "v")
        P.barrier(tiny)
        phase_p2()
        P.barrier(tiny)
        phase_p3()
        P.barrier(tiny)
        phase_p4()
        P.add("sp", lambda e: None, writes=["OUT"])
        P.emit()
    return nc


def _pc(v):
    return np.ascontiguousarray(np.asarray(v, np.float32).reshape(16, 128).T)


def _rope_table(pos):
    inv_freq = (np.float32(500000.0) ** (-np.arange(0, 32, 2, dtype=np.float32) / np.float32(32))).astype(np.float32)
    ang = (pos.astype(np.float32)[:, None] * inv_freq[None, :]).astype(np.float32)
    cos = np.cos(ang).astype(np.float32)
    sin = np.sin(ang).astype(np.float32)
    return np.concatenate([np.tile(cos, (1, 4)), np.tile(sin, (1, 4))], axis=1).astype(np.float32)


def _masks(left_real, right_real):
    kk = np.arange(128)[:, None]
    qq = np.arange(256)[None, :]
    band = ((kk >= qq - 128) & (kk <= qq)).astype(np.float32)
    m_l = band if left_real else band * (kk >= 64)
    m_r = band if right_real else band * (kk < 64)
    return np.stack([band, m_l, m_r]).astype(np.float32)


_NC_CACHE = {}


def run_layer(T, seqs, P):
    if T not in _NC_CACHE:
        _NC_CACHE[T] = build_program(T)
    nc = _NC_CACHE[T]
    conv_w = np.asarray(P["conv_w"][0], np.float32)
    conv_b = np.asarray(P["conv_b"][0], np.float32)
    w_a = np.asarray(P["lru_w_a"][0], np.float32)
    w_i = np.asarray(P["lru_w_i"][0], np.float32)
    b_a = np.asarray(P["lru_b_a"][0], np.float32)
    b_i = np.asarray(P["lru_b_i"][0], np.float32)
    lam = np.asarray(P["lru_lambda"][0], np.float32)
    zeros = np.zeros(D, np.float32)
    shared = dict(
        w_in=np.ascontiguousarray(P["w_in"][0], np.float32), w_rnn=np.ascontiguousarray(P["w_rnn_proj"][0], np.float32),
        w_att=np.ascontiguousarray(P["w_attn_proj"][0], np.float32), w_out=np.ascontiguousarray(P["w_out"][0], np.float32),
        w_up=np.ascontiguousarray(P["w_up"][0], np.float32), w_dn=np.ascontiguousarray(P["w_down"][0], np.float32),
        gvec=np.ascontiguousarray(np.concatenate([_pc(P["mix_norm_g"][0]), _pc(P["mlp_norm_g"][0])], axis=1)),
        gfin=np.ascontiguousarray(np.asarray(P["final_norm_g"], np.float32).reshape(1, D)),
    )
    NTT = T // 512 + 1
    in_maps = []
    for (xs, p0) in seqs:
        S = xs.shape[0]
        xo = np.zeros((T + 2 * HALO, D), np.float32)
        lo = max(0, p0 - HALO)
        hi = min(S, p0 + T + HALO)
        xo[lo - (p0 - HALO):hi - (p0 - HALO)] = xs[lo:hi]
        xt = np.zeros((NTT * 512, D), np.float32)
        ff = fb = 0.0
        od = 0
        w5 = [conv_w[0], conv_w[1], conv_w[2], conv_w[3], zeros]
        if p0 > 0:
            assert p0 == T
            xt[0:T] = xs[0:T]
            xt[T:T + 2] = xs[T:T + 2]
            ff = 1.0
            od = 0
        elif S > T:
            xt[0:T] = xs[T:2 * T][::-1]
            xt[T] = xs[T - 1]
            xt[T + 1] = xs[T - 2]
            fb = 1.0
            od = 1
            w5 = [zeros, conv_w[3], conv_w[2], conv_w[1], conv_w[0]]
        wg = np.stack([np.stack([w_a[0], w_a[1], w_a[od]]), np.stack([w_i[0], w_i[1], w_i[od]])]).astype(np.float32)
        cols = [_pc(conv_w[j]) for j in range(4)] + [_pc(conv_b)] + [_pc(w) for w in w5]
        cols += [_pc(b_a[0]), _pc(b_a[1]), _pc(b_a[od])] + [_pc(b_i[0]), _pc(b_i[1]), _pc(b_i[od])]
        cols += [_pc(lam[0]), _pc(lam[1]), _pc(lam[od])] + [_pc(zeros)] * 4
        prm = np.ascontiguousarray(np.concatenate(cols, axis=1))
        pos = np.arange(p0 - HALO, p0 + T + HALO)
        m = dict(shared)
        m.update(xo=xo, xt=xt, wg=np.ascontiguousarray(wg), prm=prm, cst=_rope_table(np.maximum(pos, 0)),
                 msk=_masks(p0 > 0, p0 + T < S), flg=np.ascontiguousarray(np.tile(np.array([[ff, fb]], np.float32), (128, 1))))
        in_maps.append(m)
    res = run_bass_kernel_spmd(nc, in_maps, core_ids=list(range(8)))
    return res


def kernel(x_prompt, x_sample, mix_norm_g, w_in, conv_w, conv_b, lru_w_a, lru_b_a, lru_w_i, lru_b_i, lru_lambda,
           w_rnn_proj, w_attn_proj, w_out, mlp_norm_g, w_up, w_down, final_norm_g):
    x_prompt = np.asarray(x_prompt, np.float32)
    x_sample = np.asarray(x_sample, np.float32)
    T = x_prompt.shape[1]
    assert x_prompt.shape[0] == 4 and x_sample.shape[0] == 2 and x_sample.shape[1] == 2 * T
    P = dict(mix_norm_g=np.asarray(mix_norm_g), w_in=np.asarray(w_in), conv_w=np.asarray(conv_w), conv_b=np.asarray(conv_b),
             lru_w_a=np.asarray(lru_w_a), lru_b_a=np.asarray(lru_b_a), lru_w_i=np.asarray(lru_w_i), lru_b_i=np.asarray(lru_b_i),
             lru_lambda=np.asarray(lru_lambda), w_rnn_proj=np.asarray(w_rnn_proj), w_attn_proj=np.asarray(w_attn_proj),
             w_out=np.asarray(w_out), mlp_norm_g=np.asarray(mlp_norm_g), w_up=np.asarray(w_up), w_down=np.asarray(w_down),
             final_norm_g=np.asarray(final_norm_g))
    seqs = [(x_prompt[b], 0) for b in range(4)]
    for b in range(2):
        seqs += [(x_sample[b], 0), (x_sample[b], T)]
    res = run_layer(T, seqs, P)
    outs = [np.asarray(r["y"], np.float32) for r in res.results]
    y_prompt = np.stack(outs[0:4])
    y_sample = np.stack([np.concatenate([outs[4], outs[5]]), np.concatenate([outs[6], outs[7]])])
    return (y_prompt, y_sample)
```
